# Optimizing a Trainium2 kernel written in Bass

```python
import functools
import jax, jax.numpy as jnp
from jax import lax
import numpy as np

D_MODEL = 1024
BATCH = 2
SEQ = 8192
DEPTH = 1
DEC_BATCH = 128
DEC_SEQ = 1
PAST_LEN = 16384
PAGE_SIZE = 128

PLE_DIM = 256
D_FF = 2816
MLA_HEADS = 8
MLA_NOPE = 64
MLA_ROPE = 32
MLA_V = 64
Q_LORA = 256
KV_LORA = 128
RET_HEADS = 4
RET_DK = 128
RET_DV = 128
RET_CHUNK = 128
Q_BLOCK = 128
ROPE_THETA = 10000.0
EPS = 1e-6
IN_SPLITS = (Q_LORA, KV_LORA, MLA_ROPE, RET_HEADS * RET_DK, RET_HEADS * RET_DK,
             RET_HEADS * RET_DV, RET_HEADS * RET_DV, D_MODEL, D_MODEL)
IN_COLS = Q_LORA + KV_LORA + MLA_ROPE + 2 * RET_HEADS * RET_DK + 2 * RET_HEADS * RET_DV + 2 * D_MODEL

kernel_name = 'mla_retention_gated_hybrid_step'


def _split_points():
    pts, acc = [], 0
    for s in IN_SPLITS[:-1]:
        acc += s
        pts.append(acc)
    return pts


def rmsnorm(x, g):
    xf = x.astype(jnp.float32)
    y = xf * lax.rsqrt(jnp.mean(xf * xf, axis=-1, keepdims=True) + EPS)
    return (y * g.astype(jnp.float32)).astype(x.dtype)


def head_layernorm(o, g):
    mu = jnp.mean(o, axis=-1, keepdims=True)
    d = o - mu
    var = jnp.mean(d * d, axis=-1, keepdims=True)
    return d * lax.rsqrt(var + EPS) * g.astype(jnp.float32)[None, None]


def rope(x, pos):
    half = x.shape[-1] // 2
    inv = ROPE_THETA ** (-jnp.arange(half, dtype=jnp.float32) / half)
    ang = pos.astype(jnp.float32)[:, None] * inv[None, :]
    cos = jnp.cos(ang)[None, :, None, :]
    sin = jnp.sin(ang)[None, :, None, :]
    xf = x.astype(jnp.float32)
    x1, x2 = xf[..., :half], xf[..., half:]
    return jnp.concatenate([x1 * cos - x2 * sin, x1 * sin + x2 * cos], axis=-1).astype(x.dtype)


def swiglu_half(x, g, w_gate, w_up, w_down):
    u = rmsnorm(x, g)
    return x + 0.5 * ((jax.nn.silu(u @ w_gate) * (u @ w_up)) @ w_down)


def retention_log_gamma():
    return jnp.log1p(-jnp.exp2(-5.0 - jnp.arange(RET_HEADS, dtype=jnp.float32)))


def retention_chunk(s0, q, k, v, log_gamma):
    q = q.astype(jnp.float32)
    k = k.astype(jnp.float32)
    v = v.astype(jnp.float32)
    c = q.shape[2]
    idx = jnp.arange(c, dtype=jnp.float32)
    diff = idx[:, None] - idx[None, :]
    decay = jnp.where(diff[None] >= 0, jnp.exp(log_gamma[:, None, None] * jnp.maximum(diff, 0.0)[None]), 0.0)
    scores = jnp.einsum('bhid,bhjd->bhij', q, k) * decay[None]
    inner = jnp.einsum('bhij,bhjv->bhiv', scores, v)
    q_decay = jnp.exp(log_gamma[:, None] * (idx + 1.0)[None, :])
    cross = jnp.einsum('bhid,bhdv->bhiv', q, s0) * q_decay[None, :, :, None]
    k_decay = jnp.exp(log_gamma[:, None] * (c - 1.0 - idx)[None, :])
    s_new = jnp.exp(log_gamma * c)[None, :, None, None] * s0 + jnp.einsum('bhjd,bhjv->bhdv', k * k_decay[None, :, :, None], v)
    return inner + cross, s_new


def retention_prompt(q, k, v, log_gamma):
    b, t = q.shape[:2]
    nc = t // RET_CHUNK

    def to_chunks(a):
        a = a.transpose(0, 2, 1, 3)
        a = a.reshape(b, RET_HEADS, nc, RET_CHUNK, a.shape[-1])
        return jnp.moveaxis(a, 2, 0)

    def step(s, blk):
        o, s = retention_chunk(s, blk[0], blk[1], blk[2], log_gamma)
        return s, o

    s0 = jnp.zeros((b, RET_HEADS, RET_DK, RET_DV), jnp.float32)
    s_fin, o = lax.scan(step, s0, (to_chunks(q), to_chunks(k), to_chunks(v)))
    o = jnp.moveaxis(o, 0, 2).reshape(b, RET_HEADS, t, RET_DV).transpose(0, 2, 1, 3)
    return o, s_fin


def retention_sample(q, k, v, state, log_gamma):
    o, s_new = retention_chunk(state.astype(jnp.float32), q.transpose(0, 2, 1, 3),
                               k.transpose(0, 2, 1, 3), v.transpose(0, 2, 1, 3), log_gamma)
    return o.transpose(0, 2, 1, 3), s_new


def mla_attend_prompt(q_nope, q_pe, c_kv, k_pe, w_kv_b):
    b, t = q_nope.shape[:2]
    kv = (c_kv @ w_kv_b).reshape(b, t, MLA_HEADS, MLA_NOPE + MLA_V)
    k_nope, v = kv[..., :MLA_NOPE], kv[..., MLA_NOPE:]
    nqb = t // Q_BLOCK
    qn = q_nope.reshape(b, nqb, Q_BLOCK, MLA_HEADS, MLA_NOPE).swapaxes(0, 1)
    qp = q_pe.reshape(b, nqb, Q_BLOCK, MLA_HEADS, MLA_ROPE).swapaxes(0, 1)
    kpos = jnp.arange(t)
    scale = (MLA_NOPE + MLA_ROPE) ** -0.5

    def one_block(args):
        qn_b, qp_b, bi = args
        s = (jnp.einsum('bqhd,bkhd->bhqk', qn_b, k_nope)
             + jnp.einsum('bqhr,bkr->bhqk', qp_b, k_pe)).astype(jnp.float32) * scale
        qpos = bi * Q_BLOCK + jnp.arange(Q_BLOCK)
        s = jnp.where(kpos[None, :] <= qpos[:, None], s, -jnp.inf)
        p = jax.nn.softmax(s, axis=-1).astype(v.dtype)
        return jnp.einsum('bhqk,bkhd->bqhd', p, v)

    o = lax.map(one_block, (qn, qp, jnp.arange(nqb)))
    return o.swapaxes(0, 1).reshape(b, t, MLA_HEADS * MLA_V)


def mla_attend_sample(q_nope, q_pe, c_kv, k_pe, w_kv_b, cache_ckv, cache_kpe, page_table, layer):
    b, t = q_nope.shape[:2]
    n_past = page_table.shape[1] * PAGE_SIZE
    w = w_kv_b.reshape(KV_LORA, MLA_HEADS, MLA_NOPE + MLA_V)
    w_uk, w_uv = w[..., :MLA_NOPE], w[..., MLA_NOPE:]
    q_lat = jnp.einsum('bthd,chd->bthc', q_nope, w_uk)
    past_ckv = cache_ckv[layer, page_table].reshape(b, n_past, KV_LORA).astype(c_kv.dtype)
    past_kpe = cache_kpe[layer, page_table].reshape(b, n_past, MLA_ROPE).astype(k_pe.dtype)
    scale = (MLA_NOPE + MLA_ROPE) ** -0.5
    s_past = (jnp.einsum('bthc,bkc->bhtk', q_lat, past_ckv)
              + jnp.einsum('bthr,bkr->bhtk', q_pe, past_kpe)).astype(jnp.float32) * scale
    s_new = (jnp.einsum('bthc,bkc->bhtk', q_lat, c_kv)
             + jnp.einsum('bthr,bkr->bhtk', q_pe, k_pe)).astype(jnp.float32) * scale
    tri = jnp.arange(t)[None, :] <= jnp.arange(t)[:, None]
    s_new = jnp.where(tri[None, None], s_new, -jnp.inf)
    p = jax.nn.softmax(jnp.concatenate([s_past, s_new], axis=-1), axis=-1).astype(c_kv.dtype)
    o_lat = (jnp.einsum('bhtk,bkc->bthc', p[..., :n_past], past_ckv)
             + jnp.einsum('bhtk,bkc->bthc', p[..., n_past:], c_kv))
    o = jnp.einsum('bthc,chd->bthd', o_lat, w_uv)
    return o.reshape(b, t, MLA_HEADS * MLA_V)


def hybrid_layer(x, p_emb, pos, lp, attend, retain):
    b, t = x.shape[:2]
    h = swiglu_half(x, lp['ffn1_norm'], lp['ffn1_w_gate'], lp['ffn1_w_up'], lp['ffn1_w_down'])
    u = rmsnorm(h, lp['mix_norm'])
    z = u @ lp['w_in']
    cq, ckv_raw, kpe_raw, rq, rk, rv, rg, ga, gr = jnp.split(z, _split_points(), axis=-1)
    q = (rmsnorm(cq, lp['q_a_norm']) @ lp['w_q_b']).reshape(b, t, MLA_HEADS, MLA_NOPE + MLA_ROPE)
    q_nope = q[..., :MLA_NOPE]
    q_pe = rope(q[..., MLA_NOPE:], pos)
    c_kv = rmsnorm(ckv_raw, lp['kv_a_norm'])
    k_pe = rope(kpe_raw[:, :, None, :], pos)[:, :, 0, :]
    o_att = attend(q_nope, q_pe, c_kv, k_pe)
    qr = rope(rq.reshape(b, t, RET_HEADS, RET_DK), pos)
    kr = rope(rk.reshape(b, t, RET_HEADS, RET_DK), pos) * (RET_DK ** -0.5)
    vr = rv.reshape(b, t, RET_HEADS, RET_DV)
    o_ret, s_new = retain(qr, kr, vr)
    o_ret = head_layernorm(o_ret, lp['ret_norm']).astype(x.dtype).reshape(b, t, RET_HEADS * RET_DV)
    o_ret = jax.nn.silu(rg) * o_ret
    merged = (jax.nn.sigmoid(ga) * (o_att @ lp['w_branch_att'])
              + jax.nn.sigmoid(gr) * (o_ret @ lp['w_branch_ret']))
    h = h + merged @ lp['w_out']
    h = swiglu_half(h, lp['ffn2_norm'], lp['ffn2_w_gate'], lp['ffn2_w_up'], lp['ffn2_w_down'])
    gate = jax.nn.sigmoid(rmsnorm(h, lp['ple_norm']) @ lp['w_ple_gate'])
    h = h + gate * (p_emb @ lp['w_ple_proj'])
    return h, c_kv, k_pe, s_new


def setup_inputs(seed: int = 0) -> dict:
    key = jax.random.key(seed)
    ks = iter(jax.random.split(key, 40))
    n_pages = PAST_LEN // PAGE_SIZE
    n_used = DEC_BATCH * n_pages
    n_pool = n_used + n_used // 4

    def w(shape, fan_in):
        return jax.random.normal(next(ks), shape, jnp.float32) * (fan_in ** -0.5)

    def gain(shape):
        return 1.0 + 0.02 * jax.random.normal(next(ks), shape, jnp.float32)

    d = {}
    d['x_prompt'] = jax.random.normal(next(ks), (BATCH, SEQ, D_MODEL), jnp.float32)
    d['x_sample'] = jax.random.normal(next(ks), (DEC_BATCH, DEC_SEQ, D_MODEL), jnp.float32)
    d['cache_ckv'] = jax.random.normal(next(ks), (DEPTH, n_pool, PAGE_SIZE, KV_LORA), jnp.float32)
    d['cache_kpe'] = jax.random.normal(next(ks), (DEPTH, n_pool, PAGE_SIZE, MLA_ROPE), jnp.float32)
    d['state_ret'] = 0.5 * jax.random.normal(next(ks), (DEPTH, DEC_BATCH, RET_HEADS, RET_DK, RET_DV), jnp.float32)
    d['page_table'] = jax.random.permutation(next(ks), n_pool)[:n_used].reshape(DEC_BATCH, n_pages).astype(jnp.int32)
    d['p_prompt'] = jax.random.normal(next(ks), (DEPTH, BATCH, SEQ, PLE_DIM), jnp.float32)
    d['p_sample'] = jax.random.normal(next(ks), (DEPTH, DEC_BATCH, DEC_SEQ, PLE_DIM), jnp.float32)
    d['ffn1_norm'] = gain((DEPTH, D_MODEL))
    d['ffn1_w_gate'] = w((DEPTH, D_MODEL, D_FF), D_MODEL)
    d['ffn1_w_up'] = w((DEPTH, D_MODEL, D_FF), D_MODEL)
    d['ffn1_w_down'] = w((DEPTH, D_FF, D_MODEL), D_FF)
    d['mix_norm'] = gain((DEPTH, D_MODEL))
    d['w_in'] = w((DEPTH, D_MODEL, IN_COLS), D_MODEL)
    d['q_a_norm'] = gain((DEPTH, Q_LORA))
    d['w_q_b'] = w((DEPTH, Q_LORA, MLA_HEADS * (MLA_NOPE + MLA_ROPE)), Q_LORA)
    d['kv_a_norm'] = gain((DEPTH, KV_LORA))
    d['w_kv_b'] = w((DEPTH, KV_LORA, MLA_HEADS * (MLA_NOPE + MLA_V)), KV_LORA)
    d['ret_norm'] = gain((DEPTH, RET_HEADS, RET_DV))
    d['w_branch_att'] = w((DEPTH, MLA_HEADS * MLA_V, D_MODEL), MLA_HEADS * MLA_V)
    d['w_branch_ret'] = w((DEPTH, RET_HEADS * RET_DV, D_MODEL), RET_HEADS * RET_DV)
    d['w_out'] = w((DEPTH, D_MODEL, D_MODEL), D_MODEL)
    d['ffn2_norm'] = gain((DEPTH, D_MODEL))
    d['ffn2_w_gate'] = w((DEPTH, D_MODEL, D_FF), D_MODEL)
    d['ffn2_w_up'] = w((DEPTH, D_MODEL, D_FF), D_MODEL)
    d['ffn2_w_down'] = w((DEPTH, D_FF, D_MODEL), D_FF)
    d['ple_norm'] = gain((DEPTH, D_MODEL))
    d['w_ple_gate'] = w((DEPTH, D_MODEL, D_MODEL), D_MODEL)
    d['w_ple_proj'] = w((DEPTH, PLE_DIM, D_MODEL), PLE_DIM)
    d['final_norm'] = gain((D_MODEL,))
    return d


def reference(x_prompt, x_sample, cache_ckv, cache_kpe, state_ret, page_table, p_prompt, p_sample,
              ffn1_norm, ffn1_w_gate, ffn1_w_up, ffn1_w_down, mix_norm, w_in, q_a_norm, w_q_b,
              kv_a_norm, w_kv_b, ret_norm, w_branch_att, w_branch_ret, w_out,
              ffn2_norm, ffn2_w_gate, ffn2_w_up, ffn2_w_down, ple_norm, w_ple_gate, w_ple_proj,
              final_norm):
    n_past = page_table.shape[1] * PAGE_SIZE
    pos_p = jnp.arange(x_prompt.shape[1], dtype=jnp.int32)
    pos_s = n_past + jnp.arange(x_sample.shape[1], dtype=jnp.int32)
    log_gamma = retention_log_gamma()
    hp, hs = x_prompt, x_sample
    ckv_p, kpe_p, ret_p, ckv_s, kpe_s, ret_s = [], [], [], [], [], []
    for i in range(DEPTH):
        lp = {
            'ffn1_norm': ffn1_norm[i], 'ffn1_w_gate': ffn1_w_gate[i], 'ffn1_w_up': ffn1_w_up[i],
            'ffn1_w_down': ffn1_w_down[i], 'mix_norm': mix_norm[i], 'w_in': w_in[i],
            'q_a_norm': q_a_norm[i], 'w_q_b': w_q_b[i], 'kv_a_norm': kv_a_norm[i],
            'ret_norm': ret_norm[i], 'w_branch_att': w_branch_att[i], 'w_branch_ret': w_branch_ret[i],
            'w_out': w_out[i], 'ffn2_norm': ffn2_norm[i], 'ffn2_w_gate': ffn2_w_gate[i],
            'ffn2_w_up': ffn2_w_up[i], 'ffn2_w_down': ffn2_w_down[i], 'ple_norm': ple_norm[i],
            'w_ple_gate': w_ple_gate[i], 'w_ple_proj': w_ple_proj[i],
        }
        attend_p = functools.partial(mla_attend_prompt, w_kv_b=w_kv_b[i])
        retain_p = functools.partial(retention_prompt, log_gamma=log_gamma)
        hp, c1, k1, s1 = hybrid_layer(hp, p_prompt[i], pos_p, lp, attend_p, retain_p)
        attend_s = functools.partial(mla_attend_sample, w_kv_b=w_kv_b[i], cache_ckv=cache_ckv,
                                     cache_kpe=cache_kpe, page_table=page_table, layer=i)
        retain_s = functools.partial(retention_sample, state=state_ret[i], log_gamma=log_gamma)
        hs, c2, k2, s2 = hybrid_layer(hs, p_sample[i], pos_s, lp, attend_s, retain_s)
        ckv_p.append(c1); kpe_p.append(k1); ret_p.append(s1)
        ckv_s.append(c2); kpe_s.append(k2); ret_s.append(s2)
    y_prompt = rmsnorm(hp, final_norm)
    y_sample = rmsnorm(hs, final_norm)
    new_ckv_prompt = jnp.stack(ckv_p, axis=0)
    new_kpe_prompt = jnp.stack(kpe_p, axis=0)
    new_ret_prompt = jnp.stack(ret_p, axis=0)
    new_ckv_sample = jnp.stack(ckv_s, axis=0)
    new_kpe_sample = jnp.stack(kpe_s, axis=0)
    new_ret_sample = jnp.stack(ret_s, axis=0)
    return (y_prompt, y_sample, new_ckv_prompt, new_kpe_prompt, new_ret_prompt,
            new_ckv_sample, new_kpe_sample, new_ret_sample)
```

```python
import contextlib
import numpy as np
import ml_dtypes
import concourse.bass as bass
import concourse.mybir as mybir
from concourse.bass_utils import run_bass_kernel_spmd

F32 = mybir.dt.float32
BF16 = mybir.dt.bfloat16
I32 = mybir.dt.int32
ALU = mybir.AluOpType
AF = mybir.ActivationFunctionType
AX = mybir.AxisListType

CE = ('pe', 'act', 'dve', 'pool')
ENGS = ('pe', 'act', 'dve', 'pool', 'sp')
NDMA = 24


class Buf:
    __slots__ = ('w', 'r', 'name', 'excl')

    def __init__(self, name='', excl=False):
        self.w = None
        self.r = []
        self.name = name
        self.excl = excl


class Sched:
    def __init__(self, same_eng_sync=True):
        self.prog = {e: [] for e in ENGS}
        self.cnt = {e: 0 for e in CE}
        self.seen = {e: {} for e in ENGS}
        self.dma_slot = {'sp': 0, 'pool': 0, 'act': 0}
        self.dma_uses = {}
        self.same_eng_sync = same_eng_sync
        self.ncc = 0
        self.all_dma_events = []

    def _clock(self, eng):
        s = self.seen[eng]
        return tuple(s.get(e, 0) for e in CE)

    def _need(self, eng, ev, skip_same=False):
        key, val, clock = ev
        s = self.seen[eng]
        if key == eng and (eng == 'pe' or skip_same or not self.same_eng_sync):
            return
        if s.get(key, 0) >= val:
            return
        self.prog[eng].append(('wait', key, val))
        s[key] = val
        if clock is not None:
            for e, v in zip(CE, clock):
                if s.get(e, 0) < v:
                    s[e] = v

    def _deps(self, eng, r, w):
        for b in r:
            if b.w is not None:
                self._need(eng, b.w)
            if b.excl:
                for ev in b.r:
                    self._need(eng, ev, skip_same=True)
        for b in w:
            if b.w is not None:
                self._need(eng, b.w)
            for ev in b.r:
                self._need(eng, ev)

    def _record(self, ev, r, w):
        for b in r:
            b.r = [x for x in b.r if x[0] != ev[0]]
            b.r.append(ev)
        for b in w:
            b.w = ev
            b.r = []

    def op(self, eng, fn, r=(), w=(), sig=True):
        self._deps(eng, r, w)
        if sig:
            self.cnt[eng] += 1
            val = self.cnt[eng]
        else:
            val = self.cnt[eng] + 1
        self.prog[eng].append(('op', fn, sig))
        ev = (eng, val, self._clock(eng))
        self._record(ev, r, w)
        return ev

    def dma(self, q, fn, r=(), w=()):
        self._deps(q, r, w)
        k = self.dma_slot[q]
        self.dma_slot[q] = (k + 1) % NDMA
        key = ('dma', q, k)
        uses = self.dma_uses.get(key, 0)
        if uses > 0:
            self._need(q, (key, 16 * uses, None))
        self.dma_uses[key] = uses + 1
        val = 16 * (uses + 1)
        self.prog[q].append(('dma', fn, key))
        ev = (key, val, self._clock(q))
        self._record(ev, r, w)
        self.all_dma_events.append(ev)
        return ev

    def cc(self, fn, r=(), w=()):
        q = 'pool'
        self._deps(q, r, w)
        key = ('cc', self.ncc)
        self.ncc += 1
        self.prog[q].append(('cc', fn, key))
        ev = (key, 1, self._clock(q))
        self._record(ev, r, w)
        self.all_dma_events.append(ev)
        return ev

    def latest_dma(self):
        best = {}
        for ev in self.all_dma_events:
            if ev[0] not in best or best[ev[0]][1] < ev[1]:
                best[ev[0]] = ev
        return list(best.values())

    def finish(self):
        for ev in self.latest_dma():
            self._need('sp', ev)
        for e in CE:
            if self.cnt[e] > 0:
                self._need('sp', (e, self.cnt[e], None))

    def emit(self, nc):
        keys = set()
        for e in ENGS:
            for it in self.prog[e]:
                if it[0] == 'wait':
                    keys.add(it[1])
                elif it[0] in ('dma', 'cc'):
                    keys.add(it[2])
        for e in CE:
            keys.add(e)
        keys = sorted(keys, key=str)
        with contextlib.ExitStack() as st:
            sems = {}
            for i, k in enumerate(keys):
                sems[k] = st.enter_context(nc.semaphore('s%d' % i))
            block = st.enter_context(nc.Block())
            prog = self.prog

            def run(engname, eng):
                mysem = sems.get(engname)
                for it in prog[engname]:
                    if it[0] == 'wait':
                        eng.wait_ge(sems[it[1]], it[2])
                    elif it[0] == 'op':
                        ins = it[1](eng)
                        if it[2]:
                            ins.then_inc(mysem, 1)
                    elif it[0] == 'dma':
                        it[1](eng).then_inc(sems[it[2]], 16)
                    elif it[0] == 'cc':
                        it[1](eng).then_inc(sems[it[2]])

            @block.tensor
            def _(eng):
                run('pe', eng)

            @block.scalar
            def _(eng):
                run('act', eng)

            @block.vector
            def _(eng):
                run('dve', eng)

            @block.gpsimd
            def _(eng):
                run('pool', eng)

            @block.sync
            def _(eng):
                run('sp', eng)


D = 1024
FF = 2816
NFF = 22
TOKP = 2048
NS = 16
TOK = TOKP + NS
NT = 17
EPS = 1e-6
FFN_PARTS = [(0, 4), (4, 4), (8, 4), (12, 4), (16, 3), (19, 3)]
SLOT = 6 * 3072
LG = [float(np.log1p(-2.0 ** (-5.0 - h))) for h in range(4)]


def tsz(t):
    return 128 if t < 16 else NS


class _Stop(Exception):
    pass


def build(with_cache=True):
    import os
    KSTOP = os.environ.get('KSTOP', 'full')
    nc = bass.Bass("TRN2", target_bir_lowering=False)
    S = Sched()

    def din(name, shape, dt=F32):
        return nc.dram_tensor(name, shape, dt, kind="ExternalInput").ap()

    def dout(name, shape, dt=F32):
        return nc.dram_tensor(name, shape, dt, kind="ExternalOutput").ap()

    xin = din("xin", [TOK, D])
    pin = din("pin", [TOK, 256])
    ident_d = din("ident", [128, 128], BF16)
    tabm_d = din("tabm", [TOK, 64])
    tabr_d = din("tabr", [TOK, 256])
    kdt_d = din("kdt", [144, 4])
    qdtab_d = din("qdtab", [128, 512])
    dtab_d = din("dtab", [128, 512])
    gtab_d = din("gtab", [128, 32])
    eyeT_d = din("eyeT", [128, 272])
    amask_d = din("amask", [16, 128, 512], BF16)
    state_d = din("state", [NS, 4, 128, 128])
    ptT_d = din("ptT", [128, NS], I32)
    if with_cache:
        cckv_d = din("cache_ckv", [20480, 16384])
        ckpe_d = din("cache_kpe", [20480, 4096])
    W = {}
    for nm, shp in [("ffn1_norm", [1, D]), ("ffn1_w_gate", [D, FF]), ("ffn1_w_up", [D, FF]), ("ffn1_w_down", [FF, D]),
                    ("mix_norm", [1, D]), ("w_in", [D, 4512]), ("q_a_norm", [1, 256]), ("w_q_b", [256, 768]),
                    ("kv_a_norm", [1, 128]), ("w_kv_b", [128, 1024]), ("ret_norm", [1, 512]),
                    ("w_branch_att", [512, D]), ("w_branch_ret", [512, D]), ("w_out", [D, D]),
                    ("ffn2_norm", [1, D]), ("ffn2_w_gate", [D, FF]), ("ffn2_w_up", [D, FF]), ("ffn2_w_down", [FF, D]),
                    ("ple_norm", [1, D]), ("w_ple_gate", [D, D]), ("w_ple_proj", [256, D]), ("final_norm", [1, D])]:
        W[nm] = din(nm, shp)

    y_o = dout("y", [TOK, D])
    ckv_o = dout("ckv_new", [TOK, 128])
    kpe_o = dout("kpe_new", [TOK, 32])
    retp_o = dout("ret_p", [4, 128, 128])
    rets_o = dout("ret_s", [NS, 4, 128, 128])

    cc1_in = nc.dram_tensor("cc1_in", [160, TOKP], BF16).ap()
    cc1_out = nc.dram_tensor("cc1_out", [4 * 160, TOKP], BF16).ap()
    cc2_in = nc.dram_tensor("cc2_in", [4 * 128, 512], F32).ap()
    cc2_out = nc.dram_tensor("cc2_out", [16 * 128, 512], F32).ap()
    h_d = nc.dram_tensor("h_scr", [TOK, D], F32).ap()
    qr_d = nc.dram_tensor("qr_scr", [TOK, 512], BF16).ap()
    kd_d = nc.dram_tensor("kd_scr", [TOK, 512], BF16).ap()
    v_d = nc.dram_tensor("v_scr", [TOK, 512], BF16).ap()
    rg_d = nc.dram_tensor("rg_scr", [TOK, 512], BF16).ap()
    hdb = [Buf() for _ in range(NT)]
    scrb = [Buf() for _ in range(NT)]
    cc1b, cc1ob, cc2b, cc2ob = Buf(), Buf(), Buf(), Buf()

    with contextlib.ExitStack() as st:
        BIG = st.enter_context(nc.sbuf_tensor("big", [128, 106000], BF16))

        def V16(off, nel):
            return BIG[:, off // 2: off // 2 + nel]

        def V32(off, nel, dt=F32):
            return BIG[:, off // 2: off // 2 + 2 * nel].bitcast(dt)

        KB = 1024
        O_CONST, O_RING, O_UT, O_MULTI, O_H, O_SPARE = 0, 8 * KB, 58 * KB, 91 * KB + 512, 133 * KB, 201 * KB
        ident = V16(0, 128 * 128 // 128)[:, 0:128]
        identb = Buf()
        gkv = V32(256, 128); gkvb = Buf()
        gq_col = V32(768, 2); gqb = Buf()
        gr_col = V32(776, 4); grb = Buf()
        gtab = V32(800, 32); gtabb = Buf()
        kdt = V32(928, 4); kdts = V32(944, 4); kdtb = Buf()
        ss = V32(1024, 8 * NT)
        rs = V32(1024 + 4 * 8 * NT, 8 * NT)
        tabm = V32(2304, NT * 64).rearrange("p (t f) -> p t f", t=NT); tabmb = Buf()
        st4 = V32(6656, 64)
        ckvS_T = V16(6912, 16)
        kpeS_T = V16(6944, 16)
        ckvS_tok = V16(6976, 128)
        eyeS = V32(7232, 16)
        smpb = Buf()
        ring = [V16(O_RING, 12 * KB), V16(O_RING + 24 * KB, 12 * KB)]
        ringb = [Buf(), Buf()]
        uT = V16(O_UT, 8 * TOK).rearrange("p (k t) -> p k t", k=8)
        uTb = [Buf() for _ in range(NT)]
        h = [V32(O_H + 4 * KB * t, D) for t in range(NT)]
        hb = [Buf() for _ in range(NT)]
        cqT = V16(O_MULTI, 2 * TOK).rearrange("p (k t) -> p k t", k=2)
        cqTb = [Buf() for _ in range(NT)]
        o_retT = V16(O_MULTI + 8256, 4 * TOK).rearrange("p (k t) -> p k t", k=4)
        o_attT = V16(O_MULTI + 8256 + 16512, 4 * TOK).rearrange("p (k t) -> p k t", k=4)
        oretb, oattb = Buf(), Buf()

        pA = [st.enter_context(nc.psum_tensor("pA%d" % i, [128, 512], F32)) for i in range(4)]
        pAb = [Buf(excl=True) for _ in range(4)]
        pO = [st.enter_context(nc.psum_tensor("pO%d" % i, [128, 512], F32)) for i in range(2)]
        pOb = [Buf(excl=True) for _ in range(2)]
        pT = [st.enter_context(nc.psum_tensor("pT%d" % i, [128, 8, 128], BF16)) for i in range(2)]
        pTb = [Buf(excl=True) for _ in range(2)]
        rot = {}

        def nxt(name, n):
            i = rot.get(name, 0)
            rot[name] = (i + 1) % n
            return i

        def getA():
            i = nxt('pA', 4)
            return pA[i][:, :], pAb[i]

        def getO():
            i = nxt('pO', 2)
            return pO[i][:, :], pOb[i]

        def getT():
            i = nxt('pT', 2)
            return pT[i][:, :, :], pTb[i]

        outb = Buf()
        ssbd = {}

        def ssB(col):
            if col not in ssbd:
                ssbd[col] = Buf()
            return ssbd[col]

        def barrier():
            for e in ENGS:
                for e2 in CE:
                    if S.cnt[e2] > 0:
                        S._need(e, (e2, S.cnt[e2], None))
                for ev in S.latest_dma():
                    S._need(e, ev)

        def mm(out, lhsT, rhs, start, stop, r, w, sgc=False):
            if sgc:
                S.op('pe', lambda e: e.matmul(out, lhsT=lhsT, rhs=rhs, start=start, stop=stop, skip_group_check=True),
                     r=r, w=w, sig=stop)
            else:
                S.op('pe', lambda e: e.matmul(out, lhsT=lhsT, rhs=rhs, start=start, stop=stop), r=r, w=w, sig=stop)

        def tr(out, in_, n, r, w, sig):
            S.op('pe', lambda e: e.transpose(out=out, in_=in_, identity=ident[:n, :n]), r=list(r) + [identb], w=w, sig=sig)

        def ACT(out, in_, func, r, w, **kw):
            S.op('act', lambda e: e.activation(out=out, in_=in_, func=func, **kw), r=r, w=w)

        def AMUL(out, in_, mul, r, w):
            S.op('act', lambda e: e.mul(out=out, in_=in_, mul=mul), r=r, w=w)

        def CP(eng, out, in_, r, w):
            if eng == 'act':
                S.op('act', lambda e: e.copy(out=out, in_=in_), r=r, w=w)
            else:
                S.op(eng, lambda e: e.tensor_copy(out=out, in_=in_), r=r, w=w)

        def TT(eng, out, in0, in1, op, r, w):
            S.op(eng, lambda e: e.tensor_tensor(out=out, in0=in0, in1=in1, op=op), r=r, w=w)

        def STT(eng, out, in0, scalar, in1, op0, op1, r, w):
            S.op(eng, lambda e: e.scalar_tensor_tensor(out=out, in0=in0, scalar=scalar, in1=in1, op0=op0, op1=op1), r=r, w=w)

        def TS(eng, out, in0, s1, s2, op0, op1, r, w):
            if s2 is None:
                S.op(eng, lambda e: e.tensor_scalar(out=out, in0=in0, scalar1=s1, scalar2=None, op0=op0), r=r, w=w)
            else:
                S.op(eng, lambda e: e.tensor_scalar(out=out, in0=in0, scalar1=s1, scalar2=s2, op0=op0, op1=op1), r=r, w=w)

        def RED(eng, out, in_, r, w):
            S.op(eng, lambda e: e.tensor_reduce(out=out, in_=in_, axis=AX.X, op=ALU.add), r=r, w=w)

        def MEMSET(eng, ap, val, w):
            S.op(eng, lambda e: e.memset(ap, val), r=[], w=w)

        def DMA(q, out, in_, r, w):
            S.dma(q, lambda e: e.dma_start(out=out, in_=in_), r=r, w=w)

        def load_bc(dst, dstb, src, width):
            DMA('sp', dst[:, 0:width], src.partition_broadcast(128).rearrange("p a f -> p (a f)"), [], [dstb])

        def sumsq(junk, in_, n, width, col, r):
            ACT(junk[:n, 0:width], in_, AF.Square, r, [ssB(col)], accum_out=ss[:n, col:col + 1])

        def rstd_ops(col, n, dim):
            TS('dve', rs[:n, col:col + 1], ss[:n, col:col + 1], 1.0 / dim, EPS, ALU.mult, ALU.add, [ssB(col)], [ssB(col)])
            ACT(rs[:n, col:col + 1], rs[:n, col:col + 1], AF.Sqrt, [ssB(col)], [ssB(col)])
            S.op('dve', lambda e: e.reciprocal(out=rs[:n, col:col + 1], in_=rs[:n, col:col + 1]), r=[ssB(col)], w=[ssB(col)])

        blocks = [(0, 4), (4, 4), (8, 4), (12, 4), (16, 1)]

        def blk_tok(b):
            t0, ntl = blocks[b]
            return 128 * t0, (512 if ntl == 4 else NS)

        wl = [0]

        def ring_next():
            i = wl[0] % 2
            wl[0] += 1
            return ring[i], ringb[i]

        def ffn_phase(wg, wu, wd, gname, nidx):
            o = O_MULTI
            gbc = V32(o, D); o += 4 * KB
            junk = V16(o, D); o += 2 * KB
            xn = [V16(o, D), V16(o + 2 * KB, D)]; o += 4 * KB
            sg = [V16(o, 512), V16(o + KB, 512)]; o += 2 * KB
            actT = [V16(o, 4 * 512).rearrange("p (f t) -> p f t", f=4), V16(o + 4 * KB, 4 * 512).rearrange("p (f t) -> p f t", f=4)]
            o += 8 * KB
            gbcb, xnb, sgb, actTb = Buf(), [Buf(), Buf()], [Buf(), Buf()], [Buf(), Buf()]
            load_bc(gbc, gbcb, W[gname], D)

            def load_part(part):
                f0, ng = part
                R, rb = ring_next()
                gv = R[:, 0:8 * ng * 128].rearrange("p (k f) -> p k f", k=8)
                uv = R[:, 8 * ng * 128:16 * ng * 128].rearrange("p (k f) -> p k f", k=8)
                dv = R[:, 16 * ng * 128:16 * ng * 128 + ng * D].rearrange("p (f d) -> p f d", f=ng)
                DMA('pool', gv, wg.rearrange("(k p) f -> p k f", p=128)[:, :, f0 * 128:(f0 + ng) * 128], [], [rb])
                DMA('pool', uv, wu.rearrange("(k p) f -> p k f", p=128)[:, :, f0 * 128:(f0 + ng) * 128], [], [rb])
                DMA('pool', dv, wd.rearrange("(f p) d -> p f d", p=128)[:, f0:f0 + ng, :], [], [rb])
                return gv, uv, dv, rb

            def norm_T(t):
                n = tsz(t)
                t0 = 128 * t
                col = nidx * NT + t
                sumsq(junk, h[t][:n, :], n, D, col, [hb[t]])
                rstd_ops(col, n, D)
                i = nxt('xn', 2)
                STT('dve', xn[i][:n, :], h[t][:n, :], rs[:n, col:col + 1], gbc[:n, :], ALU.mult, ALU.mult,
                    [hb[t], ssB(col), gbcb], [xnb[i]])
                p, pb = getT()
                for c in range(8):
                    tr(p[:, c, 0:n], xn[i][:n, c * 128:(c + 1) * 128], n, [xnb[i]], [pb], c == 7)
                CP('act', uT[:, :, t0:t0 + n], p[:, :, 0:n], [pb], [uTb[t]])

            loaded = [load_part(FFN_PARTS[0])]
            for pi, part in enumerate(FFN_PARTS):
                gv, uv, dv, rb = loaded[pi]
                if pi + 1 < len(FFN_PARTS):
                    loaded.append(load_part(FFN_PARTS[pi + 1]))
                f0, ng = part
                for b in range(5):
                    bt0, bn = blk_tok(b)
                    if pi == 0 and b == 0:
                        for t in range(0, 4):
                            norm_T(t)
                    ai = b % 2
                    tiles = list(range(blocks[b][0], blocks[b][0] + blocks[b][1]))
                    ub = [uTb[t] for t in tiles]
                    for f in range(ng):
                        pg, pgb = getA()
                        pu, pub = getA()
                        for k in range(8):
                            mm(pg[:, 0:bn], gv[:, k, f * 128:(f + 1) * 128], uT[:, k, bt0:bt0 + bn], k == 0, k == 7, ub + [rb], [pgb])
                        for k in range(8):
                            mm(pu[:, 0:bn], uv[:, k, f * 128:(f + 1) * 128], uT[:, k, bt0:bt0 + bn], k == 0, k == 7, ub + [rb], [pub])
                        si = nxt('sg', 2)
                        ACT(sg[si][:, 0:bn], pg[:, 0:bn], AF.Silu, [pgb], [sgb[si]])
                        TT('dve', actT[ai][:, f, 0:bn], sg[si][:, 0:bn], pu[:, 0:bn], ALU.mult, [sgb[si], pub], [actTb[ai]])
                    if pi == 0 and b + 1 < 5:
                        for t in range(blocks[b + 1][0], blocks[b + 1][0] + blocks[b + 1][1]):
                            norm_T(t)
                    for ti, t in enumerate(tiles):
                        n = tsz(t)
                        for oh in range(2):
                            pd, pdb = getA()
                            for f in range(ng):
                                mm(pd[:n, :], actT[ai][:, f, ti * 128:ti * 128 + n], dv[:, f, oh * 512:(oh + 1) * 512],
                                   f == 0, f == ng - 1, [actTb[ai], rb], [pdb])
                            hs = h[t][:n, oh * 512:(oh + 1) * 512]
                            STT('dve', hs, pd[:n, :], 0.5, hs, ALU.mult, ALU.add, [pdb], [hb[t]])
            barrier()

        try:
            DMA('sp', ident, ident_d, [], [identb])
            MEMSET('dve', ss, 0.0, [])
            for t in range(NT):
                n = tsz(t)
                DMA('sp', h[t][:n, :], xin[128 * t:128 * t + n, :], [], [hb[t]])
            load_bc(gkv, gkvb, W["kv_a_norm"], 128)
            for c in range(2):
                DMA('sp', gq_col[:, c:c + 1], W["q_a_norm"][0:1, c * 128:(c + 1) * 128].rearrange("a p -> p a"), [], [gqb])
            for c in range(4):
                DMA('sp', gr_col[:, c:c + 1], W["ret_norm"][0:1, c * 128:(c + 1) * 128].rearrange("a p -> p a"), [], [grb])
            DMA('sp', gtab, gtab_d, [], [gtabb])
            DMA('sp', kdt, kdt_d[0:128, :], [], [kdtb])
            DMA('sp', kdts[:NS, :], kdt_d[128:144, :], [], [kdtb])
            for t in range(NT):
                n = tsz(t)
                DMA('sp', tabm[:n, t, :], tabm_d[128 * t:128 * t + n, :], [], [tabmb])
            barrier()

            ffn_phase(W["ffn1_w_gate"], W["ffn1_w_up"], W["ffn1_w_down"], "ffn1_norm", 0)

            def mix_norm_all(src_from_dram, gname="mix_norm", nidx=None):
                o = O_SPARE
                gbc = V32(o, D); gbcb = Buf()
                o2 = O_MULTI + 41 * KB - 12 * KB if False else None
                load_bc(gbc, gbcb, W[gname], D)
                junk = ring[1][:, 0:D]
                xn = [ring[1][:, D:2 * D], ring[1][:, 2 * D:3 * D]]
                xnb = [Buf(), Buf()]
                hst = [V32(O_RING + 24 * KB + 6 * KB, D), V32(O_RING + 24 * KB + 10 * KB, D)]
                hstb = [Buf(), Buf()]
                for t in range(NT):
                    n = tsz(t)
                    t0 = 128 * t
                    col = (nidx if nidx is not None else (1 if not src_from_dram else 5)) * NT + t
                    if src_from_dram:
                        k = nxt('hst', 2)
                        src, srcb = hst[k], hstb[k]
                        DMA('sp', src[:n, :], h_d[t0:t0 + n, :], [hdb[t]], [srcb])
                    else:
                        src, srcb = h[t], hb[t]
                    sumsq(junk, src[:n, :], n, D, col, [srcb])
                    rstd_ops(col, n, D)
                    i = nxt('xnm', 2)
                    STT('dve', xn[i][:n, :], src[:n, :], rs[:n, col:col + 1], gbc[:n, :], ALU.mult, ALU.mult,
                        [srcb, ssB(col), gbcb], [xnb[i]])
                    p, pb = getT()
                    for c in range(8):
                        tr(p[:, c, 0:n], xn[i][:n, c * 128:(c + 1) * 128], n, [xnb[i]], [pb], c == 7)
                    CP('act', uT[:, :, t0:t0 + n], p[:, :, 0:n], [pb], [uTb[t]])
                barrier()

            mix_norm_all(False)
            for t in range(NT):
                n = tsz(t)
                DMA('sp', h_d[128 * t:128 * t + n, :], h[t][:n, :], [hb[t]], [hdb[t]])
            barrier()

            o = O_H
            tabr = V32(o, 256 * 2).rearrange("p (i f) -> p i f", i=2); o += 2 * KB
            tabrb = [Buf(), Buf()]
            kvT_own = V16(o, TOKP); o += 4 * KB
            kpT_own = V16(o, TOKP); o += 4 * KB
            kvTb = Buf()
            junk = V16(o, 512); o += KB
            stg_f = [V32(o, 160), V32(o + 640, 160)]; o += 1280
            stg_fb = [Buf(), Buf()]
            stg_b = [V16(o, 416), V16(o + 832, 416)]; o += 1664
            stg_bb = [Buf(), Buf()]
            tmp32 = [V32(o, 32), V32(o + 128, 32)]; o += 256
            tmp32b = [Buf(), Buf()]
            tA = [V32(o, 512), V32(o + 2 * KB, 512)]; o += 4 * KB
            tAb = [Buf(), Buf()]
            tB = [V32(o, 512), V32(o + 2 * KB, 512)]; o += 4 * KB
            tBb = [Buf(), Buf()]
            ob16 = [V16(o + KB * i, 512) for i in range(4)]; o += 4 * KB
            ob16b = [Buf() for _ in range(4)]

            Rw, rwb = ring_next()
            wsm = Rw[:, 0:8 * 416].rearrange("p (k f) -> p k f", k=8)
            DMA('pool', wsm, W["w_in"].rearrange("(k p) f -> p k f", p=128)[:, :, 0:416], [], [rwb])
            for t in range(NT):
                n = tsz(t)
                t0 = 128 * t
                pz, pzb = getA()
                for k in range(8):
                    mm(pz[:n, 0:416], uT[:, k, t0:t0 + n], wsm[:, k, :], k == 0, k == 7, [uTb[t], rwb], [pzb])
                c_q = 2 * NT + t
                c_kv = 3 * NT + t
                sumsq(junk, pz[:n, 0:256], n, 256, c_q, [pzb])
                sumsq(junk, pz[:n, 256:384], n, 128, c_kv, [pzb])
                rstd_ops(c_q, n, 256)
                rstd_ops(c_kv, n, 128)
                fi = nxt('stgf', 2)
                bi = nxt('stgb', 2)
                ti = nxt('tmp32', 2)
                sf, sfb, sbf, sbfb, tm, tmb = stg_f[fi], stg_fb[fi], stg_b[bi], stg_bb[bi], tmp32[ti], tmp32b[ti]
                TS('dve', sbf[:n, 0:256], pz[:n, 0:256], rs[:n, c_q:c_q + 1], None, ALU.mult, None, [pzb, ssB(c_q)], [sbfb])
                STT('dve', sf[:n, 0:128], pz[:n, 256:384], rs[:n, c_kv:c_kv + 1], gkv[:n, :], ALU.mult, ALU.mult,
                    [pzb, ssB(c_kv), gkvb], [sfb])
                TT('dve', sf[:n, 128:160], pz[:n, 384:416], tabm[:n, t, 0:32], ALU.mult, [pzb, tabmb], [sfb])
                TT('dve', tm[:n, 0:16], pz[:n, 400:416], tabm[:n, t, 32:48], ALU.mult, [pzb, tabmb], [tmb])
                TT('dve', tm[:n, 16:32], pz[:n, 384:400], tabm[:n, t, 48:64], ALU.mult, [pzb, tabmb], [tmb])
                TT('dve', sf[:n, 128:160], sf[:n, 128:160], tm[:n, :], ALU.add, [tmb], [sfb])
                DMA('sp', ckv_o[t0:t0 + n, :], sf[:n, 0:128], [sfb], [outb])
                DMA('sp', kpe_o[t0:t0 + n, :], sf[:n, 128:160], [sfb], [outb])
                CP('act', sbf[:n, 256:416], sf[:n, 0:160], [sfb], [sbfb])
                p, pb = getT()
                tr(p[:, 0, 0:n], sbf[:n, 0:128], n, [sbfb], [pb], False)
                tr(p[:, 1, 0:n], sbf[:n, 128:256], n, [sbfb], [pb], False)
                tr(p[:, 2, 0:n], sbf[:n, 256:384], n, [sbfb], [pb], False)
                tr(p[0:32, 3, 0:n], sbf[:n, 384:416], n, [sbfb], [pb], True)
                for c in range(2):
                    AMUL(cqT[:, c, t0:t0 + n], p[:, c, 0:n], gq_col[:, c:c + 1], [pb, gqb], [cqTb[t]])
                if t < 16:
                    CP('act', kvT_own[:, t0:t0 + n], p[:, 2, 0:n], [pb], [kvTb])
                    CP('act', kpT_own[0:32, t0:t0 + n], p[0:32, 3, 0:n], [pb], [kvTb])
                else:
                    CP('act', ckvS_T[:, 0:NS], p[:, 2, 0:NS], [pb], [smpb])
                    CP('act', kpeS_T[0:32, 0:NS], p[0:32, 3, 0:NS], [pb], [smpb])
                    CP('dve', ckvS_tok[:NS, :], sbf[:NS, 256:384], [sbfb], [smpb])
            DMA('sp', cc1_in[0:128, :], kvT_own[:, :], [kvTb], [cc1b])
            DMA('sp', cc1_in[128:160, :], kpT_own[0:32, :], [kvTb], [cc1b])
            S.cc(lambda e: e.collective_compute("AllGather", ALU.bypass, replica_groups=[[0, 1, 2, 3], [4, 5, 6, 7]],
                                                ins=[cc1_in], outs=[cc1_out]), r=[cc1b], w=[cc1ob])

            Rw, rwb = ring_next()
            wq = Rw[:, 0:8 * 1024].rearrange("p (k f) -> p k f", k=8)
            DMA('pool', wq, W["w_in"].rearrange("(k p) f -> p k f", p=128)[:, :, 416:1440], [], [rwb])
            for t in range(NT):
                n = tsz(t)
                t0 = 128 * t
                ri = nxt('tabr', 2)
                DMA('sp', tabr[:n, ri, :], tabr_d[t0:t0 + n, :], [], [tabrb[ri]])
                for which in range(2):
                    pz, pzb = getA()
                    for k in range(8):
                        mm(pz[:n, :], uT[:, k, t0:t0 + n], wq[:, k, which * 512:(which + 1) * 512], k == 0, k == 7, [uTb[t], rwb], [pzb])
                    X = pz[:n, :].rearrange("p (h d) -> p h d", h=4)
                    ai = nxt('tA', 2)
                    A3 = tA[ai][:n, :].rearrange("p (h d) -> p h d", h=4)
                    B3 = tB[ai][:n, :].rearrange("p (h d) -> p h d", h=4)
                    cc_ = tabr[:n, ri, 0:128].unsqueeze(1).to_broadcast([n, 4, 128])
                    ms_ = tabr[:n, ri, 128:192].unsqueeze(1).to_broadcast([n, 4, 64])
                    ps_ = tabr[:n, ri, 192:256].unsqueeze(1).to_broadcast([n, 4, 64])
                    TT('dve', A3, X, cc_, ALU.mult, [pzb, tabrb[ri]], [tAb[ai]])
                    TT('dve', B3[:, :, 0:64], X[:, :, 64:128], ms_, ALU.mult, [pzb, tabrb[ri]], [tBb[ai]])
                    TT('dve', B3[:, :, 64:128], X[:, :, 0:64], ps_, ALU.mult, [pzb, tabrb[ri]], [tBb[ai]])
                    oi = nxt('ob16', 4)
                    if which == 0:
                        TT('pool', ob16[oi][:n, :], tA[ai][:n, :], tB[ai][:n, :], ALU.add, [tAb[ai], tBb[ai]], [ob16b[oi]])
                        DMA('sp', qr_d[t0:t0 + n, :], ob16[oi][:n, :], [ob16b[oi]], [scrb[t]])
                    else:
                        TT('pool', tA[ai][:n, :], tA[ai][:n, :], tB[ai][:n, :], ALU.add, [tBb[ai]], [tAb[ai]])
                        kd_ = (kdt if t < 16 else kdts)[:n, :].unsqueeze(2).to_broadcast([n, 4, 128])
                        TT('pool', ob16[oi][:n, :].rearrange("p (h d) -> p h d", h=4), A3, kd_, ALU.mult, [tAb[ai], kdtb], [ob16b[oi]])
                        DMA('sp', kd_d[t0:t0 + n, :], ob16[oi][:n, :], [ob16b[oi]], [scrb[t]])
            Rw, rwb = ring_next()
            wv = Rw[:, 0:8 * 1024].rearrange("p (k f) -> p k f", k=8)
            DMA('pool', wv, W["w_in"].rearrange("(k p) f -> p k f", p=128)[:, :, 1440:2464], [], [rwb])
            for t in range(NT):
                n = tsz(t)
                t0 = 128 * t
                for which in range(2):
                    pz, pzb = getA()
                    for k in range(8):
                        mm(pz[:n, :], uT[:, k, t0:t0 + n], wv[:, k, which * 512:(which + 1) * 512], k == 0, k == 7, [uTb[t], rwb], [pzb])
                    oi = nxt('ob16', 4)
                    if which == 0:
                        CP('act', ob16[oi][:n, :], pz[:n, :], [pzb], [ob16b[oi]])
                        DMA('sp', v_d[t0:t0 + n, :], ob16[oi][:n, :], [ob16b[oi]], [scrb[t]])
                    else:
                        ACT(ob16[oi][:n, :], pz[:n, :], AF.Silu, [pzb], [ob16b[oi]])
                        DMA('sp', rg_d[t0:t0 + n, :], ob16[oi][:n, :], [ob16b[oi]], [scrb[t]])
            barrier()

            o = O_H
            qdtab = V32(o, 512); o += 2 * KB
            dtab = V32(o, 512); o += 2 * KB
            eyeT = V32(o, 272); o += 2 * KB
            rtabb = Buf()
            DMA('sp', qdtab, qdtab_d, [], [rtabb])
            DMA('sp', dtab, dtab_d, [], [rtabb])
            DMA('sp', eyeT, eyeT_d, [], [rtabb])
            ld = []
            for i in range(2):
                ld.append(dict(q=V16(o, 512), k=V16(o + KB, 512), v=V16(o + 2 * KB, 512), g=V16(o + 3 * KB, 512), b=Buf()))
                o += 4 * KB
            Sacc = [V32(o + 2 * KB * j, 512) for j in range(4)]; o += 8 * KB
            Saccb = [Buf() for _ in range(4)]
            Tm = V32(o, 512); o += 2 * KB
            Tmb = Buf()
            Sin = [V32(o, 512), V32(o + 2 * KB, 512)]; o += 4 * KB
            Sinb = [Buf(), Buf()]
            Sst = V32(o, 512); o += 2 * KB
            Sbf = V16(o, 512); o += KB
            Sstb = Buf()
            tmpS = V32(o, 512); o += 2 * KB
            tmpSb = Buf()
            QT = V16(o, 512); o += KB
            QdT = V16(o, 512); o += KB
            KdT = V16(o, 512); o += KB
            qkb = Buf()
            sTm = V16(o, 512); o += KB
            sTmb = Buf()
            o_sb = V32(o, 512); o += 2 * KB
            d_sb = V32(o, 512); o += 2 * KB
            sq_sb = V32(o, 512); o += 2 * KB
            osbb = Buf()
            y16 = V16(o, 512); o += KB
            y16b = Buf()
            QmT = V16(o, 4 * 256); o += 2 * KB
            QmTb = Buf()
            Km = V16(o, 512); o += KB
            Kmb = Buf()
            inner = V32(o, 512); o += 2 * KB
            innerb = Buf()

            def load_tile(t, what):
                n = tsz(t)
                t0 = 128 * t
                L = ld[nxt('ld', 2)]
                for key, src in (('q', qr_d), ('k', kd_d), ('v', v_d), ('g', rg_d)):
                    if key in what:
                        DMA('sp', L[key][:n, :], src[t0:t0 + n, :], [scrb[t]], [L['b']])
                return L

            def h3(ap, n):
                return ap[:n, :].rearrange("p (h d) -> p h d", h=4)

            def bc4(col0, n=128):
                return gtab[:n, col0:col0 + 4].unsqueeze(2).to_broadcast([n, 4, 128])

            for j in range(4):
                for c in range(4):
                    t = 4 * j + c
                    L = load_tile(t, 'kv')
                    pU, pUb = getA()
                    for hh in range(4):
                        mm(pU[:, hh * 128:(hh + 1) * 128], L['k'][:, hh * 128:(hh + 1) * 128], L['v'][:, hh * 128:(hh + 1) * 128],
                           True, True, [L['b']], [pUb])
                    cf = bc4(12 + 4 * c)
                    if c == 0:
                        TT('dve', h3(Sacc[j], 128), h3(pU, 128), cf, ALU.mult, [pUb, gtabb], [Saccb[j]])
                    else:
                        TT('dve', h3(tmpS, 128), h3(pU, 128), cf, ALU.mult, [pUb, gtabb], [tmpSb])
                        TT('dve', Sacc[j], Sacc[j], tmpS, ALU.add, [tmpSb], [Saccb[j]])
                DMA('sp', cc2_in[128 * j:128 * j + 128, :], Sacc[j], [Saccb[j]], [cc2b])
            S.cc(lambda e: e.collective_compute("AllGather", ALU.bypass, replica_groups=[[0, 1, 2, 3], [4, 5, 6, 7]],
                                                ins=[cc2_in], outs=[cc2_out]), r=[cc2b], w=[cc2ob])
            MEMSET('dve', Tm, 0.0, [Tmb])
            for j in range(4):
                MEMSET('dve', Sacc[j], 0.0, [Saccb[j]])
            for m in range(16):
                jm, rm = m // 4, m % 4
                STT('dve', Sacc[jm], Tm, gtab[:, 28 + rm:29 + rm], Sacc[jm], ALU.mult, ALU.add, [Tmb, gtabb], [Saccb[jm]])
                if m < 15:
                    si = nxt('Sin', 2)
                    row = (rm * 4 + jm) * 128
                    DMA('sp', Sin[si], cc2_out[row:row + 128, :], [cc2ob], [Sinb[si]])
                    TT('dve', h3(Tm, 128), h3(Tm, 128), bc4(4), ALU.mult, [gtabb], [Tmb])
                    TT('dve', Tm, Tm, Sin[si], ALU.add, [Sinb[si]], [Tmb])

            def ret_out(L, n, t0, pOo, pOob, extra_r):
                CP('act', o_sb[:n, :], pOo[:n, :], [pOob] + extra_r, [osbb])
                RED('dve', st4[:n, 0:4], h3(o_sb, n), [osbb], [osbb])
                TS('dve', st4[:n, 4:8], st4[:n, 0:4], -1.0 / 128, None, ALU.mult, None, [osbb], [osbb])
                TT('dve', h3(d_sb, n), h3(o_sb, n), st4[:n, 4:8].unsqueeze(2).to_broadcast([n, 4, 128]), ALU.add, [osbb], [osbb])
                ACT(sq_sb[:n, :], d_sb[:n, :], AF.Square, [osbb], [osbb])
                RED('dve', st4[:n, 8:12], h3(sq_sb, n), [osbb], [osbb])
                TS('dve', st4[:n, 12:16], st4[:n, 8:12], 1.0 / 128, EPS, ALU.mult, ALU.add, [osbb], [osbb])
                ACT(st4[:n, 12:16], st4[:n, 12:16], AF.Sqrt, [osbb], [osbb])
                S.op('dve', lambda e: e.reciprocal(out=st4[:n, 12:16], in_=st4[:n, 12:16]), r=[osbb], w=[osbb])
                TT('dve', h3(d_sb, n), h3(d_sb, n), st4[:n, 12:16].unsqueeze(2).to_broadcast([n, 4, 128]), ALU.mult, [osbb], [osbb])
                TT('dve', y16[:n, :], d_sb[:n, :], L['g'][:n, :], ALU.mult, [osbb, L['b']], [y16b])
                p, pb = getT()
                for hh in range(4):
                    tr(p[:, hh, 0:n], y16[:n, hh * 128:(hh + 1) * 128], n, [y16b], [pb], hh == 3)
                for hh in range(4):
                    AMUL(o_retT[:, hh, t0:t0 + n], p[:, hh, 0:n], gr_col[:, hh:hh + 1], [pb, grb], [oretb])

            for j in range(4):
                CP('act', Sst, Sacc[j], [Saccb[j]], [Sstb])
                CP('dve', Sbf, Sacc[j], [Saccb[j]], [Sstb])
                for c in range(4):
                    t = 4 * j + c
                    t0 = 128 * t
                    L = load_tile(t, 'qkvg')
                    p, pb = getT()
                    for hh in range(4):
                        tr(p[:, hh, :], L['q'][:, hh * 128:(hh + 1) * 128], 128, [L['b']], [pb], False)
                    for hh in range(4):
                        tr(p[:, 4 + hh, :], L['k'][:, hh * 128:(hh + 1) * 128], 128, [L['b']], [pb], hh == 3)
                    CP('act', QT.rearrange("p (h d) -> p h d", h=4), p[:, 0:4, :], [pb], [qkb])
                    TT('dve', QdT.rearrange("p (h d) -> p h d", h=4), p[:, 0:4, :], qdtab.rearrange("p (h d) -> p h d", h=4), ALU.mult,
                       [pb, rtabb], [qkb])
                    CP('act', KdT.rearrange("p (h d) -> p h d", h=4), p[:, 4:8, :], [pb], [qkb])
                    pS, pSb = getA()
                    for hh in range(4):
                        mm(pS[:, hh * 128:(hh + 1) * 128], KdT[:, hh * 128:(hh + 1) * 128], QT[:, hh * 128:(hh + 1) * 128],
                           True, True, [qkb], [pSb])
                    TT('dve', sTm, pS, dtab, ALU.mult, [pSb, rtabb], [sTmb])
                    pOo, pOob = getA()
                    for hh in range(4):
                        sl = slice(hh * 128, (hh + 1) * 128)
                        mm(pOo[:, sl], sTm[:, sl], L['v'][:, sl], True, False, [sTmb, L['b']], [pOob])
                        mm(pOo[:, sl], QdT[:, sl], Sbf[:, sl], False, True, [qkb, Sstb], [pOob])
                    ret_out(L, 128, t0, pOo, pOob, [])
                    pU, pUb = getA()
                    for hh in range(4):
                        sl = slice(hh * 128, (hh + 1) * 128)
                        mm(pU[:, sl], L['k'][:, sl], L['v'][:, sl], True, True, [L['b']], [pUb])
                    TT('dve', h3(Sst, 128), h3(Sst, 128), bc4(0), ALU.mult, [gtabb], [Sstb])
                    TT('dve', Sst, Sst, pU, ALU.add, [pUb], [Sstb])
                    CP('act', Sbf, Sst, [], [Sstb])
            DMA('sp', retp_o.rearrange("h k v -> k h v"), Sst.rearrange("p (h d) -> p h d", h=4), [Sstb], [outb])

            L = load_tile(16, 'qkvg')
            p, pb = getT()
            for hh in range(4):
                tr(p[:, hh, 0:NS], L['q'][:NS, hh * 128:(hh + 1) * 128], NS, [L['b']], [pb], hh == 3)
            TT('dve', QmT.rearrange("p (h b t) -> p h b t", h=4, b=NS),
               p[:, 0:4, 0:NS].unsqueeze(2).to_broadcast([128, 4, NS, NS]),
               eyeT[:, 0:256].rearrange("p (b t) -> p b t", b=NS).unsqueeze(1).to_broadcast([128, 4, NS, NS]), ALU.mult, [pb, rtabb], [QmTb])
            TT('dve', tmpS[:NS, :], L['q'][:NS, :], L['k'][:NS, :], ALU.mult, [L['b']], [tmpSb])
            RED('dve', st4[:NS, 16:20], h3(tmpS, NS), [tmpSb], [tmpSb])
            TT('dve', h3(inner, NS), h3(L['v'], NS), st4[:NS, 16:20].unsqueeze(2).to_broadcast([NS, 4, 128]), ALU.mult,
               [tmpSb, L['b']], [innerb])
            pC, pCb = getO()
            QmT4 = QmT.rearrange("p (h b t) -> p h b t", h=4, b=NS)
            for b in range(NS):
                si = nxt('Sin', 2)
                DMA('sp', Sin[si].rearrange("p (h d) -> p h d", h=4), state_d[b].rearrange("h k v -> k h v"), [], [Sinb[si]])
                CP('act', Sbf, Sin[si], [Sinb[si]], [Sstb])
                for hh in range(4):
                    sl = slice(hh * 128, (hh + 1) * 128)
                    mm(pC[:NS, sl], QmT4[:, hh, b, :], Sbf[:, sl], b == 0 and hh == 0, b == NS - 1, [QmTb, Sstb], [pCb], sgc=True)
                TS('dve', Km[:NS, :], L['k'][:NS, :], eyeT[:NS, 256 + b:257 + b], None, ALU.mult, None, [L['b'], rtabb], [Kmb])
                pU, pUb = getA()
                for hh in range(4):
                    sl = slice(hh * 128, (hh + 1) * 128)
                    mm(pU[:, sl], Km[:NS, sl], L['v'][:NS, sl], True, True, [Kmb, L['b']], [pUb])
                TT('dve', h3(Sst, 128), h3(Sin[si], 128), bc4(8), ALU.mult, [Sinb[si], gtabb], [Sstb])
                TT('dve', Sst, Sst, pU, ALU.add, [pUb], [Sstb])
                DMA('sp', rets_o[b].rearrange("h k v -> k h v"), Sst.rearrange("p (h d) -> p h d", h=4), [Sstb], [outb])
            TT('dve', h3(tmpS, NS), h3(pC, NS), bc4(8, NS), ALU.mult, [pCb, gtabb], [tmpSb])
            TT('dve', inner[:NS, :], inner[:NS, :], tmpS[:NS, :], ALU.add, [tmpSb], [innerb])
            ret_out(L, NS, TOKP, inner, innerb, [])
            barrier()


            if KSTOP == 'ret':
                raise _Stop()
            QTa = V16(O_UT, 8 * TOK).rearrange("p (h t) -> p h t", h=8)
            QTb = Buf()
            RB = O_RING
            KTb_ = [V16(RB, 8192).rearrange("p (j r k) -> p j r k", j=4, r=4), V16(RB + 16 * KB, 8192).rearrange("p (j r k) -> p j r k", j=4, r=4)]
            KTbb = [Buf(), Buf()]
            Pb_ = [V16(RB + 32 * KB + KB * i, 512) for i in range(3)]
            Pbb = [Buf() for _ in range(3)]
            wkvb = V16(RB + 36 * KB, 1024); wkvbb = Buf()
            wqb = V16(RB + 38 * KB, 2 * 768).rearrange("p (k f) -> p k f", k=2); wqbb = Buf()
            rl = V32(RB + 42 * KB, 8)
            o = O_H
            ckvT = V16(o, 8192).rearrange("p (r j k) -> p r j k", r=4, j=4); o += 16 * KB
            ckvTb = Buf()
            Vb_ = [V16(o, 64 * 65).rearrange("p (t d) -> p t d", t=64), V16(o + 8320, 64 * 65).rearrange("p (t d) -> p t d", t=64)]
            o += 2 * 8320
            Vbb = [Buf(), Buf()]
            o_att = V16(o, 16 * 512).rearrange("p (t d) -> p t d", t=16); o += 16 * KB
            o_attb = Buf()
            amask = V16(o, 16 * 512).rearrange("p (m q) -> p m q", m=16); o += 16 * KB
            amaskb = Buf()
            o = O_SPARE
            qs = [V16(o, 768).rearrange("p (h d) -> p h d", h=8), V16(o + 1536, 768).rearrange("p (h d) -> p h d", h=8)]; o += 3072
            qsb = [Buf(), Buf()]
            tqa = V32(o, 256); o += KB
            tqb = V32(o, 256); o += KB
            tqab = Buf()

            DMA('pool', wqb, W["w_q_b"].rearrange("(k p) f -> p k f", p=128), [], [wqbb])
            DMA('pool', wkvb, W["w_kv_b"], [], [wkvbb])
            for m4 in range(4):
                DMA('sp', amask[:, 4 * m4:4 * m4 + 4, :], amask_d[4 * m4:4 * m4 + 4].rearrange("m p q -> p m q"), [], [amaskb])
            for r_ in range(4):
                DMA('sp', ckvT[:, r_, :, :], cc1_out[r_ * 160:r_ * 160 + 128, :].rearrange("p (j k) -> p j k", j=4), [cc1ob], [ckvTb])
            for i in range(2):
                MEMSET('dve', Vb_[i][:, :, 64:65], 1.0, [Vbb[i]])
            MEMSET('dve', o_attT[:, :, TOKP:TOK], 0.0, [oattb])

            if KSTOP == 'qsetup':
                raise _Stop()
            QSTEP = int(os.environ.get('QSTEP', '9'))
            for t in range(NT):
                n = tsz(t)
                t0 = 128 * t
                qi = nxt('qs', 2)
                for (h0, nh) in ((0, 5), (5, 3)):
                    pz, pzb = getA()
                    for k in range(2):
                        mm(pz[:n, 0:nh * 96], cqT[:, k, t0:t0 + n], wqb[:, k, h0 * 96:(h0 + nh) * 96], k == 0, k == 1, [cqTb[t], wqbb], [pzb])
                    Xv = pz[:n, 0:nh * 96].rearrange("p (h d) -> p h d", h=nh)
                    if QSTEP >= 2:
                        CP('act', qs[qi][:n, h0:h0 + nh, 0:64], Xv[:, :, 0:64], [pzb], [qsb[qi]])
                    A3 = tqa[:n, 0:nh * 32].rearrange("p (h d) -> p h d", h=nh)
                    B3 = tqb[:n, 0:nh * 32].rearrange("p (h d) -> p h d", h=nh)
                    if QSTEP >= 3:
                        TT('dve', A3, Xv[:, :, 64:96], tabm[:n, t, 0:32].unsqueeze(1).to_broadcast([n, nh, 32]), ALU.mult, [pzb, tabmb], [tqab])
                        TT('dve', B3[:, :, 0:16], Xv[:, :, 80:96], tabm[:n, t, 32:48].unsqueeze(1).to_broadcast([n, nh, 16]), ALU.mult,
                           [pzb, tabmb], [tqab])
                        TT('dve', B3[:, :, 16:32], Xv[:, :, 64:80], tabm[:n, t, 48:64].unsqueeze(1).to_broadcast([n, nh, 16]), ALU.mult,
                           [pzb, tabmb], [tqab])
                        TT('dve', qs[qi][:n, h0:h0 + nh, 64:96], A3, B3, ALU.add, [tqab], [qsb[qi]])
                if QSTEP >= 4:
                    p, pb = getT()
                    for hh in range(8):
                        tr(p[0:96, hh, 0:n], qs[qi][:n, hh, :], n, [qsb[qi]], [pb], hh == 7)
                    if QSTEP >= 5:
                        CP('act', QTa[0:96, :, t0:t0 + n], p[0:96, :, 0:n], [pb], [QTb])

            if KSTOP == 'qproj':
                raise _Stop()
            SCALE = float(96.0 ** -0.5)
            evk = [0]
            for hh in range(8):
                KT, KTb = KTb_[hh % 2], KTbb[hh % 2]
                Vv, Vvb = Vb_[hh % 2], Vbb[hh % 2]
                for r_ in range(4):
                    DMA('sp', KT[64:96, :, r_, :], cc1_out[r_ * 160 + 128:r_ * 160 + 160, :].rearrange("p (j k) -> p j k", j=4), [cc1ob], [KTb])
                for g in range(16):
                    j_, r_ = g // 4, g % 4
                    pk, pkb = getA()
                    mm(pk[0:64, :], wkvb[:, hh * 128:hh * 128 + 64], ckvT[:, r_, j_, :], True, True, [wkvbb, ckvTb], [pkb])
                    CP('act' if g % 2 == 0 else 'dve', KT[0:64, j_, r_, :], pk[0:64, :], [pkb], [KTb])
                for k8 in range(8):
                    pv, pvb = getA()
                    for i in range(8):
                        kt = k8 * 8 + i
                        g, kk = kt // 4, kt % 4
                        j_, r_ = g // 4, g % 4
                        mm(pv[:, i * 64:(i + 1) * 64], ckvT[:, r_, j_, kk * 128:(kk + 1) * 128], wkvb[:, hh * 128 + 64:hh * 128 + 128],
                           True, True, [wkvbb, ckvTb], [pvb])
                    CP('dve', Vv[:, k8 * 8:(k8 + 1) * 8, 0:64], pv.rearrange("p (i d) -> p i d", i=8), [pvb], [Vvb])
                for j in range(4):
                    po, pob = getO()
                    nk = 16 * j + 16
                    for kt in range(nk):
                        g, kk = kt // 4, kt % 4
                        j_, r_ = g // 4, g % 4
                        ps_, psb = getA()
                        mm(ps_, KT[0:96, j_, r_, kk * 128:(kk + 1) * 128], QTa[0:96, hh, j * 512:(j + 1) * 512], True, True, [KTb, QTb], [psb])
                        pi = nxt('P', 3)
                        ACT(Pb_[pi], ps_, AF.Exp, [psb], [Pbb[pi]], scale=SCALE)
                        if g >= 4 * j:
                            TT('pool', Pb_[pi], Pb_[pi], amask[:, (g - 4 * j) * 4 + kk, :], ALU.mult, [amaskb], [Pbb[pi]])
                        for qt in range(4):
                            mm(po[:, qt * 68:qt * 68 + 65], Pb_[pi][:, qt * 128:(qt + 1) * 128], Vv[:, kt, :], kt == 0 and qt == 0,
                               kt == nk - 1, [Pbb[pi], Vvb], [pob], sgc=True)
                    po3 = po[:, 0:272].rearrange("p (q d) -> p q d", q=4)
                    S.op('dve', lambda e, po3=po3: e.reciprocal(out=rl[:, 0:4], in_=po3[:, :, 64]), r=[pob], w=[o_attb])
                    TT('dve', o_att[:, 4 * j:4 * j + 4, hh * 64:(hh + 1) * 64], po3[:, :, 0:64],
                       rl[:, 0:4].unsqueeze(2).to_broadcast([128, 4, 64]), ALU.mult, [pob], [o_attb])
            for t in range(16):
                p, pb = getT()
                for c in range(4):
                    tr(p[:, c, :], o_att[:, t, c * 128:(c + 1) * 128], 128, [o_attb], [pb], c == 3)
                CP('act', o_attT[:, :, 128 * t:128 * t + 128], p[:, 0:4, :], [pb], [oattb])
            barrier()

            if KSTOP == 'attn':
                raise _Stop()

            if with_cache:
                o = O_H
                Gc = [V16(o + 4 * KB * i, 2048).rearrange("p (t c) -> p t c", t=16) for i in range(3)]; o += 12 * KB
                Gcb = [Buf() for _ in range(3)]
                Gk = [V16(o + 8 * KB * i, 4096) for i in range(2)]; o += 16 * KB
                Gkb = [Buf(), Buf()]
                kT5 = [V16(o + 1280 * i, 640).rearrange("p (s k) -> p s k", s=5) for i in range(2)]; o += 2560
                kT5b = [Buf(), Buf()]
                Ps = [V16(o + 64 * i, 32) for i in range(3)]; o += 256
                Psb = [Buf() for _ in range(3)]
                Lacc = V32(o, 32); o += 128
                Laccb = Buf()
                Lred = V32(o, 8); o += 32
                ones1 = V32(o, 8); o += 32
                rl8 = V32(o, 8); o += 32
                ptsb = V32(o, 16, I32); o += 64
                idx8 = V32(o, 16, I32); o += 64
                idxb = Buf()
                wukT = V16(o, 1024).rearrange("p (h c) -> p h c", h=8); o += 2048
                wuv = V16(o, 512).rearrange("p (h d) -> p h d", h=8); o += 1024
                qlatT = V16(o, 128).rearrange("p (h b) -> p h b", h=8); o += 256
                Qblk = V16(o, 512).rearrange("p (g b) -> p g b", b=NS); o += 1024
                olatT = V16(o, 128).rearrange("p (h b) -> p h b", h=8); o += 256
                olat = V16(o, 128); o += 256
                P2 = V16(o, 8); o += 32
                P2m = V16(o, 8); o += 32
                sconb = Buf()
                DMA('sp', ptsb, ptT_d, [], [idxb])
                DMA('sp', eyeS[:NS, :], eyeT_d[0:NS, 256:272], [], [idxb])
                S.op('dve', lambda e: e.tensor_single_scalar(out=idx8, in_=ptsb, scalar=3, op=ALU.logical_shift_left), r=[idxb], w=[idxb])
                MEMSET('dve', ones1, 1.0, [sconb])
                MEMSET('dve', Qblk, 0.0, [sconb])
                p, pb = getT()
                for hh in range(8):
                    tr(p[0:64, hh, :], wkvb[:, hh * 128:hh * 128 + 64], 128, [wkvbb], [pb], hh == 7)
                CP('act', wukT[0:64, :, :], p[0:64, :, :], [pb], [sconb])
                CP('dve', wuv, wkvb.rearrange("p (h d) -> p h d", h=8)[:, :, 64:128], [wkvbb], [sconb])
                pq, pqb = getA()
                for hh in range(8):
                    mm(pq[:, hh * NS:(hh + 1) * NS], wukT[0:64, hh, :], QTa[0:64, hh, TOKP:TOK], True, True, [sconb, QTb], [pqb])
                CP('act', qlatT, pq[:, 0:8 * NS].rearrange("p (h b) -> p h b", h=8), [pqb], [sconb])
                for g in range(4):
                    DMA('sp', Qblk[32 * g:32 * g + 32, 8 * g:8 * g + 8, :], QTa[64:96, :, TOKP:TOK], [QTb, sconb], [sconb])
                for b in range(NS):
                    MEMSET('dve', Lacc, 0.0, [Laccb])
                    gk, gkb = Gk[b % 2], Gkb[b % 2]
                    S.dma('pool', lambda e, gk=gk, b=b: e.indirect_dma_start(
                        out=gk, out_offset=None, in_=ckpe_d[:, :],
                        in_offset=bass.IndirectOffsetOnAxis(ap=ptsb[:, b:b + 1], axis=0)), r=[idxb], w=[gkb])
                    gk3 = gk.rearrange("p (t r) -> p t r", t=32)
                    pol, polb = getO()
                    for ch in range(8):
                        ci = nxt('Gc', 3)
                        gc, gcb = Gc[ci], Gcb[ci]
                        S.dma('pool', lambda e, gc=gc, b=b, ch=ch: e.indirect_dma_start(
                            out=gc.rearrange("p t c -> p (t c)"), out_offset=None, in_=cckv_d[:, 0:2048], element_offset=ch * 2048,
                            in_offset=bass.IndirectOffsetOnAxis(ap=idx8[:, b:b + 1], axis=0)), r=[idxb], w=[gcb])
                        for tg in range(4):
                            p, pb = getT()
                            for i in range(4):
                                tr(p[:, i, :], gc[:, tg * 4 + i, :], 128, [gcb], [pb], False)
                            tr(p[:, 4, :], gk3[:, ch * 4 + tg, :], 128, [gkb], [pb], True)
                            ki = nxt('kT5', 2)
                            k5, k5b = kT5[ki], kT5b[ki]
                            CP('act' if tg % 2 == 0 else 'dve', k5, p[:, 0:5, :], [pb], [k5b])
                            ps_, psb = getA()
                            mm(ps_[:, 0:32], k5[:, 4, :], Qblk[:, :, b], True, False, [k5b, sconb], [psb], sgc=True)
                            for i in range(4):
                                mm(ps_[:, i * 8:(i + 1) * 8], k5[:, i, :], qlatT[:, :, b], False, i == 3, [k5b, sconb], [psb], sgc=True)
                            pi = nxt('Ps', 3)
                            ACT(Ps[pi], ps_[:, 0:32], AF.Exp, [psb], [Psb[pi]], scale=SCALE)
                            TT('dve', Lacc, Lacc, Ps[pi], ALU.add, [Psb[pi]], [Laccb])
                            for i in range(4):
                                mm(pol[0:8, 0:128], Ps[pi][:, i * 8:(i + 1) * 8], gc[:, tg * 4 + i, :],
                                   ch == 0 and tg == 0 and i == 0, False, [Psb[pi], gcb], [polb], sgc=True)
                    ps2, ps2b = getA()
                    mm(ps2[0:NS, 0:8], ckvS_T[:, 0:NS], qlatT[:, :, b], True, False, [smpb, sconb], [ps2b], sgc=True)
                    mm(ps2[0:NS, 0:8], kpeS_T[0:32, 0:NS], Qblk[0:32, 0:8, b], False, True, [smpb, sconb], [ps2b], sgc=True)
                    ACT(P2[:NS, :], ps2[0:NS, 0:8], AF.Exp, [ps2b], [sconb], scale=SCALE)
                    TS('dve', P2m[:NS, :], P2[:NS, :], eyeS[:NS, b:b + 1], None, ALU.mult, None, [sconb, idxb], [sconb])
                    TT('dve', Lacc[:NS, 0:8], Lacc[:NS, 0:8], P2m[:NS, :], ALU.add, [sconb], [Laccb])
                    mm(pol[0:8, 0:128], P2m[:NS, :], ckvS_tok[:NS, :], False, True, [sconb, smpb], [polb], sgc=True)
                    RED('dve', Lred, Lacc.rearrange("p (t h) -> p h t", t=4), [Laccb], [Laccb])
                    pl, plb = getA()
                    mm(pl[0:8, 0:1], Lred, ones1[:, 0:1], True, True, [Laccb, sconb], [plb])
                    S.op('dve', lambda e, pl=pl: e.reciprocal(out=rl8[0:8, 0:1], in_=pl[0:8, 0:1]), r=[plb], w=[sconb])
                    TS('dve', olat[0:8, :], pol[0:8, 0:128], rl8[0:8, 0:1], None, ALU.mult, None, [polb, sconb], [sconb])
                    p, pb = getT()
                    tr(p[:, 0, 0:8], olat[0:8, :], 8, [sconb], [pb], True)
                    CP('act', olatT[:, :, b], p[:, 0, 0:8], [pb], [sconb])
                for hp in range(4):
                    pf, pfb = getA()
                    mm(pf[:, 0:2 * NS], wuv[:, 2 * hp:2 * hp + 2, :].rearrange("p h d -> p (h d)"),
                       olatT[:, 2 * hp:2 * hp + 2, :].rearrange("p h b -> p (h b)"), True, True, [sconb], [pfb])
                    CP('act', o_attT[0:64, hp, TOKP:TOK], pf[0:64, 0:NS], [pfb], [oattb])
                    CP('act', o_attT[64:128, hp, TOKP:TOK], pf[64:128, NS:2 * NS], [pfb], [oattb])
                barrier()
            if KSTOP == 'samp':
                raise _Stop()
            mix_norm_all(True)
            wga = V16(RB, 8192).rearrange("p (k f) -> p k f", k=8)
            wgr = V16(RB + 16 * KB, 8192).rearrange("p (k f) -> p k f", k=8)
            wba = V16(RB + 32 * KB, 4096).rearrange("p (k f) -> p k f", k=4)
            wbr = V16(RB + 40 * KB, 4096).rearrange("p (k f) -> p k f", k=4)
            mwb = Buf()
            DMA('pool', wga, W["w_in"].rearrange("(k p) f -> p k f", p=128)[:, :, 2464:3488], [], [mwb])
            DMA('pool', wgr, W["w_in"].rearrange("(k p) f -> p k f", p=128)[:, :, 3488:4512], [], [mwb])
            DMA('pool', wba, W["w_branch_att"].rearrange("(k p) f -> p k f", p=128), [], [mwb])
            DMA('pool', wbr, W["w_branch_ret"].rearrange("(k p) f -> p k f", p=128), [], [mwb])
            o = O_H
            mergedT = V16(o, 8 * TOK).rearrange("p (k t) -> p k t", k=8); o += 33 * KB
            mergedb = [Buf() for _ in range(5)]
            sgA = [V32(o, 512), V32(o + 2 * KB, 512)]; o += 4 * KB
            sgAb = [Buf(), Buf()]
            sgR = [V32(o, 512), V32(o + 2 * KB, 512)]; o += 4 * KB
            sgRb = [Buf(), Buf()]
            hst2 = [V32(o, D), V32(o + 4 * KB, D)]; o += 8 * KB
            hst2b = [Buf(), Buf()]
            for b in range(5):
                bt0, bn = blk_tok(b)
                tiles = list(range(blocks[b][0], blocks[b][0] + blocks[b][1]))
                ub = [uTb[t] for t in tiles]
                for f in range(8):
                    fs = slice(f * 128, (f + 1) * 128)
                    pga, pgab = getA()
                    pgr, pgrb = getA()
                    pa, pab = getA()
                    pr, prb = getA()
                    for k in range(8):
                        mm(pga[:, 0:bn], wga[:, k, fs], uT[:, k, bt0:bt0 + bn], k == 0, k == 7, ub + [mwb], [pgab])
                    for k in range(8):
                        mm(pgr[:, 0:bn], wgr[:, k, fs], uT[:, k, bt0:bt0 + bn], k == 0, k == 7, ub + [mwb], [pgrb])
                    for k in range(4):
                        mm(pa[:, 0:bn], wba[:, k, fs], o_attT[:, k, bt0:bt0 + bn], k == 0, k == 3, [oattb, mwb], [pab])
                    for k in range(4):
                        mm(pr[:, 0:bn], wbr[:, k, fs], o_retT[:, k, bt0:bt0 + bn], k == 0, k == 3, [oretb, mwb], [prb])
                    si = nxt('sgA', 2)
                    ACT(sgA[si][:, 0:bn], pga[:, 0:bn], AF.Sigmoid, [pgab], [sgAb[si]])
                    ACT(sgR[si][:, 0:bn], pgr[:, 0:bn], AF.Sigmoid, [pgrb], [sgRb[si]])
                    TT('dve', sgA[si][:, 0:bn], sgA[si][:, 0:bn], pa[:, 0:bn], ALU.mult, [pab], [sgAb[si]])
                    TT('dve', sgR[si][:, 0:bn], sgR[si][:, 0:bn], pr[:, 0:bn], ALU.mult, [prb], [sgRb[si]])
                    TT('pool', mergedT[:, f, bt0:bt0 + bn], sgA[si][:, 0:bn], sgR[si][:, 0:bn], ALU.add, [sgAb[si], sgRb[si]], [mergedb[b]])
            barrier()
            wo = V16(RB, 8192).rearrange("p (k f) -> p k f", k=8)
            wob = Buf()
            DMA('pool', wo, W["w_out"].rearrange("(k p) f -> p k f", p=128), [], [wob])
            for t in range(NT):
                n = tsz(t)
                t0 = 128 * t
                b = min(t // 4, 4)
                k_ = nxt('hst2', 2)
                DMA('sp', hst2[k_][:n, :], h_d[t0:t0 + n, :], [hdb[t]], [hst2b[k_]])
                for oh in range(2):
                    po_, pob_ = getA()
                    for k in range(8):
                        mm(po_[:n, :], mergedT[:, k, t0:t0 + n], wo[:, k, oh * 512:(oh + 1) * 512], k == 0, k == 7, [mergedb[b], wob], [pob_])
                    hs = hst2[k_][:n, oh * 512:(oh + 1) * 512]
                    TT('dve', hs, hs, po_[:n, :], ALU.add, [pob_], [hst2b[k_]])
                DMA('sp', h_d[t0:t0 + n, :], hst2[k_][:n, :], [hst2b[k_]], [hdb[t]])
            barrier()

            if KSTOP == 'wout':
                raise _Stop()
            for t in range(NT):
                n = tsz(t)
                DMA('sp', h[t][:n, :], h_d[128 * t:128 * t + n, :], [hdb[t]], [hb[t]])
            ffn_phase(W["ffn2_w_gate"], W["ffn2_w_up"], W["ffn2_w_down"], "ffn2_norm", 4)

            if KSTOP == 'ffn2':
                raise _Stop()
            mix_norm_all(False, "ple_norm", 6)
            wpg = V16(RB, 8192).rearrange("p (k f) -> p k f", k=8)
            wpp = V16(RB + 16 * KB, 2048).rearrange("p (k f) -> p k f", k=2)
            wpb = Buf()
            DMA('pool', wpg, W["w_ple_gate"].rearrange("(k p) f -> p k f", p=128), [], [wpb])
            DMA('pool', wpp, W["w_ple_proj"].rearrange("(k p) f -> p k f", p=128), [], [wpb])
            o = O_MULTI
            gfin = V32(o, D); o += 4 * KB
            gfinb = Buf()
            load_bc(gfin, gfinb, W["final_norm"], D)
            pst = [V32(o, 256), V32(o + KB, 256)]; o += 2 * KB
            pstb = [Buf(), Buf()]
            pbf = [V16(o, 256), V16(o + 512, 256)]; o += KB
            pbfb = [Buf(), Buf()]
            peT = [V16(o, 256).rearrange("p (k t) -> p k t", k=2), V16(o + 512, 256).rearrange("p (k t) -> p k t", k=2)]; o += KB
            peTb = [Buf(), Buf()]
            sgP = [V32(o, 512), V32(o + 2 * KB, 512)]; o += 4 * KB
            sgPb = [Buf(), Buf()]
            junk2 = V16(o, D); o += 2 * KB
            for t in range(NT):
                n = tsz(t)
                t0 = 128 * t
                i = nxt('pst', 2)
                DMA('sp', pst[i][:n, :], pin[t0:t0 + n, :], [], [pstb[i]])
                CP('act', pbf[i][:n, :], pst[i][:n, :], [pstb[i]], [pbfb[i]])
                p, pb = getT()
                for c in range(2):
                    tr(p[:, c, 0:n], pbf[i][:n, c * 128:(c + 1) * 128], n, [pbfb[i]], [pb], c == 1)
                CP('act', peT[i][:, :, 0:n], p[:, 0:2, 0:n], [pb], [peTb[i]])
                for oh in range(2):
                    os_ = slice(oh * 512, (oh + 1) * 512)
                    pg, pgb = getA()
                    pp, ppb = getA()
                    for k in range(8):
                        mm(pg[:n, :], uT[:, k, t0:t0 + n], wpg[:, k, os_], k == 0, k == 7, [uTb[t], wpb], [pgb])
                    for k in range(2):
                        mm(pp[:n, :], peT[i][:, k, 0:n], wpp[:, k, os_], k == 0, k == 1, [peTb[i], wpb], [ppb])
                    si = nxt('sgP', 2)
                    ACT(sgP[si][:n, :], pg[:n, :], AF.Sigmoid, [pgb], [sgPb[si]])
                    TT('dve', sgP[si][:n, :], sgP[si][:n, :], pp[:n, :], ALU.mult, [ppb], [sgPb[si]])
                    TT('pool', h[t][:n, os_], h[t][:n, os_], sgP[si][:n, :], ALU.add, [sgPb[si]], [hb[t]])
                col = 7 * NT + t
                sumsq(junk2, h[t][:n, :], n, D, col, [hb[t]])
                rstd_ops(col, n, D)
                STT('dve', h[t][:n, :], h[t][:n, :], rs[:n, col:col + 1], gfin[:n, :], ALU.mult, ALU.mult, [ssB(col), gfinb], [hb[t]])
                DMA('sp', y_o[t0:t0 + n, :], h[t][:n, :], [hb[t]], [outb])

        except _Stop:
            pass
        S.finish()
        S.emit(nc)
    return nc


def _core_tokens(c):
    r = c % 4
    idx = np.concatenate([np.arange(512 * (4 * j + r), 512 * (4 * j + r) + 512) for j in range(4)])
    return c // 4, idx


def _rope_tab(pos, half):
    pos = pos.astype(np.float32)
    inv = (np.float32(10000.0) ** (-np.arange(half, dtype=np.float32) / np.float32(half))).astype(np.float32)
    ang = pos[:, None] * inv[None, :]
    c, s = np.cos(ang).astype(np.float32), np.sin(ang).astype(np.float32)
    return np.concatenate([c, c, -s, s], axis=1).astype(np.float32)


def _core_consts(c):
    r = c % 4
    b, idx = _core_tokens(c)
    pos = np.concatenate([idx, np.full(NS, 16384)])
    lg = np.array(LG, dtype=np.float64)
    i = np.arange(128, dtype=np.float64)
    sc = 128.0 ** -0.5
    kdt = np.concatenate([np.exp(lg[None, :] * (127.0 - i)[:, None]) * sc, np.full((16, 4), sc)], 0).astype(np.float32)
    qd = np.exp(lg[:, None] * (i + 1.0)[None, :])
    qdtab = np.broadcast_to(qd.reshape(1, 512), (128, 512)).astype(np.float32)
    jj = i[:, None, None]
    ii = i[None, None, :]
    dt = np.where(ii >= jj, np.exp(lg[None, :, None] * (ii - 127.0)), 0.0)
    dtab = dt.reshape(128, 512).astype(np.float32)
    g = np.zeros(32, dtype=np.float64)
    g[0:4] = np.exp(lg * 128)
    g[4:8] = np.exp(lg * 512)
    g[8:12] = np.exp(lg)
    for cc in range(4):
        g[12 + 4 * cc:16 + 4 * cc] = np.exp(lg * 128 * (3 - cc))
    g[28 + r] = 1.0
    gtab = np.broadcast_to(g[None, :], (128, 32)).astype(np.float32)
    eyeT = np.zeros((128, 272), np.float32)
    eyeT[:, 0:256] = np.eye(16, dtype=np.float32).reshape(1, 256)
    eyeT[:16, 256:272] = np.eye(16, dtype=np.float32)
    return dict(tabm=_rope_tab(pos, 16), tabr=_rope_tab(pos, 64), kdt=kdt, qdtab=qdtab, dtab=dtab, gtab=gtab, eyeT=eyeT)


_NC_CACHE = {}


def kernel(**inp):
    import os
    with_cache = os.environ.get('KNOCACHE', '') == ''
    key = with_cache
    if key not in _NC_CACHE:
        _NC_CACHE[key] = build(with_cache)
    nc = _NC_CACHE[key]
    ident = np.eye(128).astype(ml_dtypes.bfloat16)
    wnames = ["ffn1_norm", "ffn1_w_gate", "ffn1_w_up", "ffn1_w_down", "mix_norm", "w_in", "q_a_norm", "w_q_b", "kv_a_norm",
              "w_kv_b", "ret_norm", "w_branch_att", "w_branch_ret", "w_out", "ffn2_norm", "ffn2_w_gate", "ffn2_w_up",
              "ffn2_w_down", "ple_norm", "w_ple_gate", "w_ple_proj", "final_norm"]
    shared = {}
    for nm in wnames:
        a = np.asarray(inp[nm], dtype=np.float32)
        if nm == "final_norm":
            a = a.reshape(1, D)
        elif nm == "ret_norm":
            a = a.reshape(1, 512)
        else:
            a = a.reshape(a.shape[1:]) if a.ndim == 3 else a.reshape(1, -1)
        shared[nm] = np.ascontiguousarray(a)
    amask_all = []
    q = np.arange(512)[None, :]
    for r in range(4):
        m = np.zeros((4, 4, 128, 512), np.float32)
        for gp in range(4):
            for kt in range(4):
                if gp < r:
                    m[gp, kt] = 1.0
                elif gp == r:
                    kk = (kt * 128 + np.arange(128))[:, None]
                    m[gp, kt] = (q >= kk).astype(np.float32)
        amask_all.append(m.reshape(16, 128, 512).astype(ml_dtypes.bfloat16))
    in_maps = []
    for c in range(8):
        b, idx = _core_tokens(c)
        m = dict(shared)
        m["xin"] = np.ascontiguousarray(np.concatenate([inp["x_prompt"][b, idx], inp["x_sample"][NS * c:NS * c + NS, 0]], 0))
        m["pin"] = np.ascontiguousarray(np.concatenate([inp["p_prompt"][0, b, idx], inp["p_sample"][0, NS * c:NS * c + NS, 0]], 0))
        m["ident"] = ident
        m.update(_core_consts(c))
        m["amask"] = amask_all[c % 4]
        m["state"] = np.ascontiguousarray(inp["state_ret"][0, NS * c:NS * c + NS])
        m["ptT"] = np.ascontiguousarray(np.asarray(inp["page_table"])[NS * c:NS * c + NS].T.astype(np.int32))
        if with_cache:
            m["cache_ckv"] = np.asarray(inp["cache_ckv"]).reshape(20480, 16384)
            m["cache_kpe"] = np.asarray(inp["cache_kpe"]).reshape(20480, 4096)
        in_maps.append(m)
    res = run_bass_kernel_spmd(nc, in_maps, core_ids=list(range(8)))
    y_p = np.zeros((2, 8192, D), np.float32)
    y_s = np.zeros((128, 1, D), np.float32)
    ckv_p = np.zeros((1, 2, 8192, 128), np.float32)
    kpe_p = np.zeros((1, 2, 8192, 32), np.float32)
    ret_p = np.zeros((1, 2, 4, 128, 128), np.float32)
    ckv_s = np.zeros((1, 128, 1, 128), np.float32)
    kpe_s = np.zeros((1, 128, 1, 32), np.float32)
    ret_s = np.zeros((1, 128, 4, 128, 128), np.float32)
    for c in range(8):
        b, idx = _core_tokens(c)
        r = res.results[c]
        y_p[b, idx] = r["y"][:TOKP]
        y_s[NS * c:NS * c + NS, 0] = r["y"][TOKP:]
        ckv_p[0, b, idx] = r["ckv_new"][:TOKP]
        ckv_s[0, NS * c:NS * c + NS, 0] = r["ckv_new"][TOKP:]
        kpe_p[0, b, idx] = r["kpe_new"][:TOKP]
        kpe_s[0, NS * c:NS * c + NS, 0] = r["kpe_new"][TOKP:]
        ret_s[0, NS * c:NS * c + NS] = r["ret_s"]
        if c % 4 == 3:
            ret_p[0, b] = r["ret_p"]
    return (y_p, y_s, ckv_p, kpe_p, ret_p, ckv_s, kpe_s, ret_s)
```

```python
import contextlib
import numpy as np
import ml_dtypes
import concourse.bass as bass
import concourse.mybir as mybir
from concourse.bass_utils import run_bass_kernel_spmd

F32 = mybir.dt.float32
BF16 = mybir.dt.bfloat16
I32 = mybir.dt.int32
ALU = mybir.AluOpType
AF = mybir.ActivationFunctionType
AX = mybir.AxisListType

CE = ('pe', 'act', 'dve', 'pool')
ENGS = ('pe', 'act', 'dve', 'pool', 'sp')
NDMA = 24


class Buf:
    __slots__ = ('w', 'r', 'name', 'excl')

    def __init__(self, name='', excl=False):
        self.w = None
        self.r = []
        self.name = name
        self.excl = excl


class Sched:
    def __init__(self, same_eng_sync=True):
        self.prog = {e: [] for e in ENGS}
        self.cnt = {e: 0 for e in CE}
        self.seen = {e: {} for e in ENGS}
        self.dma_slot = {'sp': 0, 'pool': 0, 'act': 0}
        self.dma_uses = {}
        self.same_eng_sync = same_eng_sync
        self.ncc = 0
        self.all_dma_events = []

    def _clock(self, eng):
        s = self.seen[eng]
        return tuple(s.get(e, 0) for e in CE)

    def _need(self, eng, ev, skip_same=False):
        key, val, clock = ev
        s = self.seen[eng]
        if key == eng and (eng == 'pe' or skip_same or not self.same_eng_sync):
            return
        if s.get(key, 0) >= val:
            return
        self.prog[eng].append(('wait', key, val))
        s[key] = val
        if clock is not None:
            for e, v in zip(CE, clock):
                if s.get(e, 0) < v:
                    s[e] = v

    def _deps(self, eng, r, w):
        for b in r:
            if b.w is not None:
                self._need(eng, b.w)
            if b.excl:
                for ev in b.r:
                    self._need(eng, ev, skip_same=True)
        for b in w:
            if b.w is not None:
                self._need(eng, b.w)
            for ev in b.r:
                self._need(eng, ev)

    def _record(self, ev, r, w):
        for b in r:
            b.r = [x for x in b.r if x[0] != ev[0]]
            b.r.append(ev)
        for b in w:
            b.w = ev
            b.r = []

    def op(self, eng, fn, r=(), w=(), sig=True):
        self._deps(eng, r, w)
        if sig:
            self.cnt[eng] += 1
            val = self.cnt[eng]
        else:
            val = self.cnt[eng] + 1
        self.prog[eng].append(('op', fn, sig))
        ev = (eng, val, self._clock(eng))
        self._record(ev, r, w)
        return ev

    def dma(self, q, fn, r=(), w=()):
        self._deps(q, r, w)
        k = self.dma_slot[q]
        self.dma_slot[q] = (k + 1) % NDMA
        key = ('dma', q, k)
        uses = self.dma_uses.get(key, 0)
        if uses > 0:
            self._need(q, (key, 16 * uses, None))
        self.dma_uses[key] = uses + 1
        val = 16 * (uses + 1)
        self.prog[q].append(('dma', fn, key))
        ev = (key, val, self._clock(q))
        self._record(ev, r, w)
        self.all_dma_events.append(ev)
        return ev

    def cc(self, fn, r=(), w=()):
        q = 'pool'
        self._deps(q, r, w)
        key = ('cc', self.ncc)
        self.ncc += 1
        self.prog[q].append(('cc', fn, key))
        ev = (key, 1, self._clock(q))
        self._record(ev, r, w)
        self.all_dma_events.append(ev)
        return ev

    def latest_dma(self):
        best = {}
        for ev in self.all_dma_events:
            if ev[0] not in best or best[ev[0]][1] < ev[1]:
                best[ev[0]] = ev
        return list(best.values())

    def finish(self):
        for ev in self.latest_dma():
            self._need('sp', ev)
        for e in CE:
            if self.cnt[e] > 0:
                self._need('sp', (e, self.cnt[e], None))

    def emit(self, nc):
        keys = set()
        for e in ENGS:
            for it in self.prog[e]:
                if it[0] == 'wait':
                    keys.add(it[1])
                elif it[0] in ('dma', 'cc'):
                    keys.add(it[2])
        for e in CE:
            keys.add(e)
        keys = sorted(keys, key=str)
        with contextlib.ExitStack() as st:
            sems = {}
            for i, k in enumerate(keys):
                sems[k] = st.enter_context(nc.semaphore('s%d' % i))
            block = st.enter_context(nc.Block())
            prog = self.prog

            def run(engname, eng):
                mysem = sems.get(engname)
                for it in prog[engname]:
                    if it[0] == 'wait':
                        eng.wait_ge(sems[it[1]], it[2])
                    elif it[0] == 'op':
                        ins = it[1](eng)
                        if it[2]:
                            ins.then_inc(mysem, 1)
                    elif it[0] == 'dma':
                        it[1](eng).then_inc(sems[it[2]], 16)
                    elif it[0] == 'cc':
                        it[1](eng).then_inc(sems[it[2]])

            @block.tensor
            def _(eng):
                run('pe', eng)

            @block.scalar
            def _(eng):
                run('act', eng)

            @block.vector
            def _(eng):
                run('dve', eng)

            @block.gpsimd
            def _(eng):
                run('pool', eng)

            @block.sync
            def _(eng):
                run('sp', eng)


D = 1024
FF = 2816
NFF = 22
TOKP = 2048
NS = 16
TOK = TOKP + NS
NT = 17
EPS = 1e-6
FFN_PARTS = [(0, 4), (4, 4), (8, 4), (12, 4), (16, 3), (19, 3)]
SLOT = 6 * 3072
LG = [float(np.log1p(-2.0 ** (-5.0 - h))) for h in range(4)]


def tsz(t):
    return 128 if t < 16 else NS


class _Stop(Exception):
    pass


def build(with_cache=True):
    import os
    KSTOP = os.environ.get('KSTOP', 'full')
    nc = bass.Bass("TRN2", target_bir_lowering=False)
    S = Sched()

    def din(name, shape, dt=F32):
        return nc.dram_tensor(name, shape, dt, kind="ExternalInput").ap()

    def dout(name, shape, dt=F32):
        return nc.dram_tensor(name, shape, dt, kind="ExternalOutput").ap()

    xin = din("xin", [TOK, D])
    pin = din("pin", [TOK, 256])
    ident_d = din("ident", [128, 128], BF16)
    tabm_d = din("tabm", [TOK, 64])
    tabr_d = din("tabr", [TOK, 256])
    kdt_d = din("kdt", [144, 4])
    qdtab_d = din("qdtab", [128, 512])
    dtab_d = din("dtab", [128, 512])
    gtab_d = din("gtab", [128, 32])
    eyeT_d = din("eyeT", [128, 272])
    amask_d = din("amask", [16, 128, 512], BF16)
    state_d = din("state", [NS, 4, 128, 128])
    ptT_d = din("ptT", [128, NS], I32)
    if with_cache:
        cckv_d = din("cache_ckv", [20480, 16384])
        ckpe_d = din("cache_kpe", [20480, 4096])
    W = {}
    for nm, shp in [("ffn1_norm", [1, D]), ("ffn1_w_gate", [D, FF]), ("ffn1_w_up", [D, FF]), ("ffn1_w_down", [FF, D]),
                    ("mix_norm", [1, D]), ("w_in", [D, 4512]), ("q_a_norm", [1, 256]), ("w_q_b", [256, 768]),
                    ("kv_a_norm", [1, 128]), ("w_kv_b", [128, 1024]), ("ret_norm", [1, 512]),
                    ("w_branch_att", [512, D]), ("w_branch_ret", [512, D]), ("w_out", [D, D]),
                    ("ffn2_norm", [1, D]), ("ffn2_w_gate", [D, FF]), ("ffn2_w_up", [D, FF]), ("ffn2_w_down", [FF, D]),
                    ("ple_norm", [1, D]), ("w_ple_gate", [D, D]), ("w_ple_proj", [256, D]), ("final_norm", [1, D])]:
        W[nm] = din(nm, shp)

    y_o = dout("y", [TOK, D])
    ckv_o = dout("ckv_new", [TOK, 128])
    kpe_o = dout("kpe_new", [TOK, 32])
    retp_o = dout("ret_p", [4, 128, 128])
    rets_o = dout("ret_s", [NS, 4, 128, 128])

    cc1_in = nc.dram_tensor("cc1_in", [160, TOKP], BF16).ap()
    cc1_out = nc.dram_tensor("cc1_out", [4 * 160, TOKP], BF16).ap()
    cc2_in = nc.dram_tensor("cc2_in", [4 * 128, 512], F32).ap()
    cc2_out = nc.dram_tensor("cc2_out", [16 * 128, 512], F32).ap()
    h_d = nc.dram_tensor("h_scr", [TOK, D], F32).ap()
    qr_d = nc.dram_tensor("qr_scr", [TOK, 512], BF16).ap()
    kd_d = nc.dram_tensor("kd_scr", [TOK, 512], BF16).ap()
    v_d = nc.dram_tensor("v_scr", [TOK, 512], BF16).ap()
    rg_d = nc.dram_tensor("rg_scr", [TOK, 512], BF16).ap()
    hdb = [Buf() for _ in range(NT)]
    scrb = [Buf() for _ in range(NT)]
    cc1b, cc1ob, cc2b, cc2ob = Buf(), Buf(), Buf(), Buf()

    with contextlib.ExitStack() as st:
        BIG = st.enter_context(nc.sbuf_tensor("big", [128, 106000], BF16))

        def V16(off, nel):
            return BIG[:, off // 2: off // 2 + nel]

        def V32(off, nel, dt=F32):
            return BIG[:, off // 2: off // 2 + 2 * nel].bitcast(dt)

        KB = 1024
        O_CONST, O_RING, O_UT, O_MULTI, O_H, O_SPARE = 0, 8 * KB, 58 * KB, 91 * KB + 512, 133 * KB, 201 * KB
        ident = V16(0, 128 * 128 // 128)[:, 0:128]
        identb = Buf()
        gkv = V32(256, 128); gkvb = Buf()
        gq_col = V32(768, 2); gqb = Buf()
        gr_col = V32(776, 4); grb = Buf()
        gtab = V32(800, 32); gtabb = Buf()
        kdt = V32(928, 4); kdts = V32(944, 4); kdtb = Buf()
        ss = V32(1024, 8 * NT)
        rs = V32(1024 + 4 * 8 * NT, 8 * NT)
        tabm = V32(2304, NT * 64).rearrange("p (t f) -> p t f", t=NT); tabmb = Buf()
        st4 = V32(6656, 64)
        ckvS_T = V16(6912, 16)
        kpeS_T = V16(6944, 16)
        ckvS_tok = V16(6976, 128)
        eyeS = V32(7232, 16)
        smpb = Buf()
        ring = [V16(O_RING, 12 * KB), V16(O_RING + 24 * KB, 12 * KB)]
        ringb = [Buf(), Buf()]
        uT = V16(O_UT, 8 * TOK).rearrange("p (k t) -> p k t", k=8)
        uTb = [Buf() for _ in range(NT)]
        h = [V32(O_H + 4 * KB * t, D) for t in range(NT)]
        hb = [Buf() for _ in range(NT)]
        cqT = V16(O_MULTI, 2 * TOK).rearrange("p (k t) -> p k t", k=2)
        cqTb = [Buf() for _ in range(NT)]
        o_retT = V16(O_MULTI + 8256, 4 * TOK).rearrange("p (k t) -> p k t", k=4)
        o_attT = V16(O_MULTI + 8256 + 16512, 4 * TOK).rearrange("p (k t) -> p k t", k=4)
        oretb, oattb = Buf(), Buf()

        pA = [st.enter_context(nc.psum_tensor("pA%d" % i, [128, 512], F32)) for i in range(4)]
        pAb = [Buf(excl=True) for _ in range(4)]
        pO = [st.enter_context(nc.psum_tensor("pO%d" % i, [128, 512], F32)) for i in range(2)]
        pOb = [Buf(excl=True) for _ in range(2)]
        pT = [st.enter_context(nc.psum_tensor("pT%d" % i, [128, 8, 128], BF16)) for i in range(2)]
        pTb = [Buf(excl=True) for _ in range(2)]
        rot = {}

        def nxt(name, n):
            i = rot.get(name, 0)
            rot[name] = (i + 1) % n
            return i

        def getA():
            i = nxt('pA', 4)
            return pA[i][:, :], pAb[i]

        def getO():
            i = nxt('pO', 2)
            return pO[i][:, :], pOb[i]

        def getT():
            i = nxt('pT', 2)
            return pT[i][:, :, :], pTb[i]

        outb = Buf()
        ssbd = {}

        def ssB(col):
            if col not in ssbd:
                ssbd[col] = Buf()
            return ssbd[col]

        def barrier():
            for e in ENGS:
                for e2 in CE:
                    if S.cnt[e2] > 0:
                        S._need(e, (e2, S.cnt[e2], None))
                for ev in S.latest_dma():
                    S._need(e, ev)

        def mm(out, lhsT, rhs, start, stop, r, w, sgc=False):
            if sgc:
                S.op('pe', lambda e: e.matmul(out, lhsT=lhsT, rhs=rhs, start=start, stop=stop, skip_group_check=True),
                     r=r, w=w, sig=stop)
            else:
                S.op('pe', lambda e: e.matmul(out, lhsT=lhsT, rhs=rhs, start=start, stop=stop), r=r, w=w, sig=stop)

        def tr(out, in_, n, r, w, sig):
            S.op('pe', lambda e: e.transpose(out=out, in_=in_, identity=ident[:n, :n]), r=list(r) + [identb], w=w, sig=sig)

        def ACT(out, in_, func, r, w, **kw):
            S.op('act', lambda e: e.activation(out=out, in_=in_, func=func, **kw), r=r, w=w)

        def AMUL(out, in_, mul, r, w):
            S.op('act', lambda e: e.mul(out=out, in_=in_, mul=mul), r=r, w=w)

        def CP(eng, out, in_, r, w):
            if eng == 'act':
                S.op('act', lambda e: e.copy(out=out, in_=in_), r=r, w=w)
            else:
                S.op(eng, lambda e: e.tensor_copy(out=out, in_=in_), r=r, w=w)

        def TT(eng, out, in0, in1, op, r, w):
            S.op(eng, lambda e: e.tensor_tensor(out=out, in0=in0, in1=in1, op=op), r=r, w=w)

        def STT(eng, out, in0, scalar, in1, op0, op1, r, w):
            S.op(eng, lambda e: e.scalar_tensor_tensor(out=out, in0=in0, scalar=scalar, in1=in1, op0=op0, op1=op1), r=r, w=w)

        def TS(eng, out, in0, s1, s2, op0, op1, r, w):
            if s2 is None:
                S.op(eng, lambda e: e.tensor_scalar(out=out, in0=in0, scalar1=s1, scalar2=None, op0=op0), r=r, w=w)
            else:
                S.op(eng, lambda e: e.tensor_scalar(out=out, in0=in0, scalar1=s1, scalar2=s2, op0=op0, op1=op1), r=r, w=w)

        def RED(eng, out, in_, r, w):
            S.op(eng, lambda e: e.tensor_reduce(out=out, in_=in_, axis=AX.X, op=ALU.add), r=r, w=w)

        def MEMSET(eng, ap, val, w):
            S.op(eng, lambda e: e.memset(ap, val), r=[], w=w)

        def DMA(q, out, in_, r, w):
            S.dma(q, lambda e: e.dma_start(out=out, in_=in_), r=r, w=w)

        def load_bc(dst, dstb, src, width):
            DMA('sp', dst[:, 0:width], src.partition_broadcast(128).rearrange("p a f -> p (a f)"), [], [dstb])

        def sumsq(junk, in_, n, width, col, r):
            ACT(junk[:n, 0:width], in_, AF.Square, r, [ssB(col)], accum_out=ss[:n, col:col + 1])

        def rstd_ops(col, n, dim):
            TS('dve', rs[:n, col:col + 1], ss[:n, col:col + 1], 1.0 / dim, EPS, ALU.mult, ALU.add, [ssB(col)], [ssB(col)])
            ACT(rs[:n, col:col + 1], rs[:n, col:col + 1], AF.Sqrt, [ssB(col)], [ssB(col)])
            S.op('dve', lambda e: e.reciprocal(out=rs[:n, col:col + 1], in_=rs[:n, col:col + 1]), r=[ssB(col)], w=[ssB(col)])

        blocks = [(0, 4), (4, 4), (8, 4), (12, 4), (16, 1)]

        def blk_tok(b):
            t0, ntl = blocks[b]
            return 128 * t0, (512 if ntl == 4 else NS)

        wl = [0]

        def ring_next():
            i = wl[0] % 2
            wl[0] += 1
            return ring[i], ringb[i]

        def ffn_phase(wg, wu, wd, gname, nidx):
            o = O_MULTI
            gbc = V32(o, D); o += 4 * KB
            junk = V16(o, D); o += 2 * KB
            xn = [V16(o, D), V16(o + 2 * KB, D)]; o += 4 * KB
            sg = [V16(o, 512), V16(o + KB, 512)]; o += 2 * KB
            actT = [V16(o, 4 * 512).rearrange("p (f t) -> p f t", f=4), V16(o + 4 * KB, 4 * 512).rearrange("p (f t) -> p f t", f=4)]
            o += 8 * KB
            gbcb, xnb, sgb, actTb = Buf(), [Buf(), Buf()], [Buf(), Buf()], [Buf(), Buf()]
            load_bc(gbc, gbcb, W[gname], D)

            def load_part(part):
                f0, ng = part
                R, rb = ring_next()
                gv = R[:, 0:8 * ng * 128].rearrange("p (k f) -> p k f", k=8)
                uv = R[:, 8 * ng * 128:16 * ng * 128].rearrange("p (k f) -> p k f", k=8)
                dv = R[:, 16 * ng * 128:16 * ng * 128 + ng * D].rearrange("p (f d) -> p f d", f=ng)
                DMA('pool', gv, wg.rearrange("(k p) f -> p k f", p=128)[:, :, f0 * 128:(f0 + ng) * 128], [], [rb])
                DMA('pool', uv, wu.rearrange("(k p) f -> p k f", p=128)[:, :, f0 * 128:(f0 + ng) * 128], [], [rb])
                DMA('pool', dv, wd.rearrange("(f p) d -> p f d", p=128)[:, f0:f0 + ng, :], [], [rb])
                return gv, uv, dv, rb

            def norm_T(t):
                n = tsz(t)
                t0 = 128 * t
                col = nidx * NT + t
                sumsq(junk, h[t][:n, :], n, D, col, [hb[t]])
                rstd_ops(col, n, D)
                i = nxt('xn', 2)
                STT('dve', xn[i][:n, :], h[t][:n, :], rs[:n, col:col + 1], gbc[:n, :], ALU.mult, ALU.mult,
                    [hb[t], ssB(col), gbcb], [xnb[i]])
                p, pb = getT()
                for c in range(8):
                    tr(p[:, c, 0:n], xn[i][:n, c * 128:(c + 1) * 128], n, [xnb[i]], [pb], c == 7)
                CP('act', uT[:, :, t0:t0 + n], p[:, :, 0:n], [pb], [uTb[t]])

            loaded = [load_part(FFN_PARTS[0])]
            for pi, part in enumerate(FFN_PARTS):
                gv, uv, dv, rb = loaded[pi]
                if pi + 1 < len(FFN_PARTS):
                    loaded.append(load_part(FFN_PARTS[pi + 1]))
                f0, ng = part
                for b in range(5):
                    bt0, bn = blk_tok(b)
                    if pi == 0 and b == 0:
                        for t in range(0, 4):
                            norm_T(t)
                    ai = b % 2
                    tiles = list(range(blocks[b][0], blocks[b][0] + blocks[b][1]))
                    ub = [uTb[t] for t in tiles]
                    for f in range(ng):
                        pg, pgb = getA()
                        pu, pub = getA()
                        for k in range(8):
                            mm(pg[:, 0:bn], gv[:, k, f * 128:(f + 1) * 128], uT[:, k, bt0:bt0 + bn], k == 0, k == 7, ub + [rb], [pgb])
                        for k in range(8):
                            mm(pu[:, 0:bn], uv[:, k, f * 128:(f + 1) * 128], uT[:, k, bt0:bt0 + bn], k == 0, k == 7, ub + [rb], [pub])
                        si = nxt('sg', 2)
                        ACT(sg[si][:, 0:bn], pg[:, 0:bn], AF.Silu, [pgb], [sgb[si]])
                        TT('dve', actT[ai][:, f, 0:bn], sg[si][:, 0:bn], pu[:, 0:bn], ALU.mult, [sgb[si], pub], [actTb[ai]])
                    if pi == 0 and b + 1 < 5:
                        for t in range(blocks[b + 1][0], blocks[b + 1][0] + blocks[b + 1][1]):
                            norm_T(t)
                    for ti, t in enumerate(tiles):
                        n = tsz(t)
                        for oh in range(2):
                            pd, pdb = getA()
                            for f in range(ng):
                                mm(pd[:n, :], actT[ai][:, f, ti * 128:ti * 128 + n], dv[:, f, oh * 512:(oh + 1) * 512],
                                   f == 0, f == ng - 1, [actTb[ai], rb], [pdb])
                            hs = h[t][:n, oh * 512:(oh + 1) * 512]
                            STT('dve', hs, pd[:n, :], 0.5, hs, ALU.mult, ALU.add, [pdb], [hb[t]])
            barrier()

        try:
            DMA('sp', ident, ident_d, [], [identb])
            MEMSET('dve', ss, 0.0, [])
            for t in range(NT):
                n = tsz(t)
                DMA('sp', h[t][:n, :], xin[128 * t:128 * t + n, :], [], [hb[t]])
            load_bc(gkv, gkvb, W["kv_a_norm"], 128)
            for c in range(2):
                DMA('sp', gq_col[:, c:c + 1], W["q_a_norm"][0:1, c * 128:(c + 1) * 128].rearrange("a p -> p a"), [], [gqb])
            for c in range(4):
                DMA('sp', gr_col[:, c:c + 1], W["ret_norm"][0:1, c * 128:(c + 1) * 128].rearrange("a p -> p a"), [], [grb])
            DMA('sp', gtab, gtab_d, [], [gtabb])
            DMA('sp', kdt, kdt_d[0:128, :], [], [kdtb])
            DMA('sp', kdts[:NS, :], kdt_d[128:144, :], [], [kdtb])
            for t in range(NT):
                n = tsz(t)
                DMA('sp', tabm[:n, t, :], tabm_d[128 * t:128 * t + n, :], [], [tabmb])
            barrier()

            ffn_phase(W["ffn1_w_gate"], W["ffn1_w_up"], W["ffn1_w_down"], "ffn1_norm", 0)

            def mix_norm_all(src_from_dram, gname="mix_norm", nidx=None):
                o = O_SPARE
                gbc = V32(o, D); gbcb = Buf()
                o2 = O_MULTI + 41 * KB - 12 * KB if False else None
                load_bc(gbc, gbcb, W[gname], D)
                junk = ring[1][:, 0:D]
                xn = [ring[1][:, D:2 * D], ring[1][:, 2 * D:3 * D]]
                xnb = [Buf(), Buf()]
                hst = [V32(O_RING + 24 * KB + 6 * KB, D), V32(O_RING + 24 * KB + 10 * KB, D)]
                hstb = [Buf(), Buf()]
                for t in range(NT):
                    n = tsz(t)
                    t0 = 128 * t
                    col = (nidx if nidx is not None else (1 if not src_from_dram else 5)) * NT + t
                    if src_from_dram:
                        k = nxt('hst', 2)
                        src, srcb = hst[k], hstb[k]
                        DMA('sp', src[:n, :], h_d[t0:t0 + n, :], [hdb[t]], [srcb])
                    else:
                        src, srcb = h[t], hb[t]
                    sumsq(junk, src[:n, :], n, D, col, [srcb])
                    rstd_ops(col, n, D)
                    i = nxt('xnm', 2)
                    STT('dve', xn[i][:n, :], src[:n, :], rs[:n, col:col + 1], gbc[:n, :], ALU.mult, ALU.mult,
                        [srcb, ssB(col), gbcb], [xnb[i]])
                    p, pb = getT()
                    for c in range(8):
                        tr(p[:, c, 0:n], xn[i][:n, c * 128:(c + 1) * 128], n, [xnb[i]], [pb], c == 7)
                    CP('act', uT[:, :, t0:t0 + n], p[:, :, 0:n], [pb], [uTb[t]])
                barrier()

            mix_norm_all(False)
            for t in range(NT):
                n = tsz(t)
                DMA('sp', h_d[128 * t:128 * t + n, :], h[t][:n, :], [hb[t]], [hdb[t]])
            barrier()

            o = O_H
            tabr = V32(o, 256 * 2).rearrange("p (i f) -> p i f", i=2); o += 2 * KB
            tabrb = [Buf(), Buf()]
            kvT_own = V16(o, TOKP); o += 4 * KB
            kpT_own = V16(o, TOKP); o += 4 * KB
            kvTb = Buf()
            junk = V16(o, 512); o += KB
            stg_f = [V32(o, 160), V32(o + 640, 160)]; o += 1280
            stg_fb = [Buf(), Buf()]
            stg_b = [V16(o, 416), V16(o + 832, 416)]; o += 1664
            stg_bb = [Buf(), Buf()]
            tmp32 = [V32(o, 32), V32(o + 128, 32)]; o += 256
            tmp32b = [Buf(), Buf()]
            tA = [V32(o, 512), V32(o + 2 * KB, 512)]; o += 4 * KB
            tAb = [Buf(), Buf()]
            tB = [V32(o, 512), V32(o + 2 * KB, 512)]; o += 4 * KB
            tBb = [Buf(), Buf()]
            ob16 = [V16(o + KB * i, 512) for i in range(4)]; o += 4 * KB
            ob16b = [Buf() for _ in range(4)]

            Rw, rwb = ring_next()
            wsm = Rw[:, 0:8 * 416].rearrange("p (k f) -> p k f", k=8)
            DMA('pool', wsm, W["w_in"].rearrange("(k p) f -> p k f", p=128)[:, :, 0:416], [], [rwb])
            for t in range(NT):
                n = tsz(t)
                t0 = 128 * t
                pz, pzb = getA()
                for k in range(8):
                    mm(pz[:n, 0:416], uT[:, k, t0:t0 + n], wsm[:, k, :], k == 0, k == 7, [uTb[t], rwb], [pzb])
                c_q = 2 * NT + t
                c_kv = 3 * NT + t
                sumsq(junk, pz[:n, 0:256], n, 256, c_q, [pzb])
                sumsq(junk, pz[:n, 256:384], n, 128, c_kv, [pzb])
                rstd_ops(c_q, n, 256)
                rstd_ops(c_kv, n, 128)
                fi = nxt('stgf', 2)
                bi = nxt('stgb', 2)
                ti = nxt('tmp32', 2)
                sf, sfb, sbf, sbfb, tm, tmb = stg_f[fi], stg_fb[fi], stg_b[bi], stg_bb[bi], tmp32[ti], tmp32b[ti]
                TS('dve', sbf[:n, 0:256], pz[:n, 0:256], rs[:n, c_q:c_q + 1], None, ALU.mult, None, [pzb, ssB(c_q)], [sbfb])
                STT('dve', sf[:n, 0:128], pz[:n, 256:384], rs[:n, c_kv:c_kv + 1], gkv[:n, :], ALU.mult, ALU.mult,
                    [pzb, ssB(c_kv), gkvb], [sfb])
                TT('dve', sf[:n, 128:160], pz[:n, 384:416], tabm[:n, t, 0:32], ALU.mult, [pzb, tabmb], [sfb])
                TT('dve', tm[:n, 0:16], pz[:n, 400:416], tabm[:n, t, 32:48], ALU.mult, [pzb, tabmb], [tmb])
                TT('dve', tm[:n, 16:32], pz[:n, 384:400], tabm[:n, t, 48:64], ALU.mult, [pzb, tabmb], [tmb])
                TT('dve', sf[:n, 128:160], sf[:n, 128:160], tm[:n, :], ALU.add, [tmb], [sfb])
                DMA('sp', ckv_o[t0:t0 + n, :], sf[:n, 0:128], [sfb], [outb])
                DMA('sp', kpe_o[t0:t0 + n, :], sf[:n, 128:160], [sfb], [outb])
                CP('act', sbf[:n, 256:416], sf[:n, 0:160], [sfb], [sbfb])
                p, pb = getT()
                tr(p[:, 0, 0:n], sbf[:n, 0:128], n, [sbfb], [pb], False)
                tr(p[:, 1, 0:n], sbf[:n, 128:256], n, [sbfb], [pb], False)
                tr(p[:, 2, 0:n], sbf[:n, 256:384], n, [sbfb], [pb], False)
                tr(p[0:32, 3, 0:n], sbf[:n, 384:416], n, [sbfb], [pb], True)
                for c in range(2):
                    AMUL(cqT[:, c, t0:t0 + n], p[:, c, 0:n], gq_col[:, c:c + 1], [pb, gqb], [cqTb[t]])
                if t < 16:
                    CP('act', kvT_own[:, t0:t0 + n], p[:, 2, 0:n], [pb], [kvTb])
                    CP('act', kpT_own[0:32, t0:t0 + n], p[0:32, 3, 0:n], [pb], [kvTb])
                else:
                    CP('act', ckvS_T[:, 0:NS], p[:, 2, 0:NS], [pb], [smpb])
                    CP('act', kpeS_T[0:32, 0:NS], p[0:32, 3, 0:NS], [pb], [smpb])
                    CP('dve', ckvS_tok[:NS, :], sbf[:NS, 256:384], [sbfb], [smpb])
            DMA('sp', cc1_in[0:128, :], kvT_own[:, :], [kvTb], [cc1b])
            DMA('sp', cc1_in[128:160, :], kpT_own[0:32, :], [kvTb], [cc1b])
            S.cc(lambda e: e.collective_compute("AllGather", ALU.bypass, replica_groups=[[0, 1, 2, 3], [4, 5, 6, 7]],
                                                ins=[cc1_in], outs=[cc1_out]), r=[cc1b], w=[cc1ob])

            Rw, rwb = ring_next()
            wq = Rw[:, 0:8 * 1024].rearrange("p (k f) -> p k f", k=8)
            DMA('pool', wq, W["w_in"].rearrange("(k p) f -> p k f", p=128)[:, :, 416:1440], [], [rwb])
            for t in range(NT):
                n = tsz(t)
                t0 = 128 * t
                ri = nxt('tabr', 2)
                DMA('sp', tabr[:n, ri, :], tabr_d[t0:t0 + n, :], [], [tabrb[ri]])
                for which in range(2):
                    pz, pzb = getA()
                    for k in range(8):
                        mm(pz[:n, :], uT[:, k, t0:t0 + n], wq[:, k, which * 512:(which + 1) * 512], k == 0, k == 7, [uTb[t], rwb], [pzb])
                    X = pz[:n, :].rearrange("p (h d) -> p h d", h=4)
                    ai = nxt('tA', 2)
                    A3 = tA[ai][:n, :].rearrange("p (h d) -> p h d", h=4)
                    B3 = tB[ai][:n, :].rearrange("p (h d) -> p h d", h=4)
                    cc_ = tabr[:n, ri, 0:128].unsqueeze(1).to_broadcast([n, 4, 128])
                    ms_ = tabr[:n, ri, 128:192].unsqueeze(1).to_broadcast([n, 4, 64])
                    ps_ = tabr[:n, ri, 192:256].unsqueeze(1).to_broadcast([n, 4, 64])
                    TT('dve', A3, X, cc_, ALU.mult, [pzb, tabrb[ri]], [tAb[ai]])
                    TT('dve', B3[:, :, 0:64], X[:, :, 64:128], ms_, ALU.mult, [pzb, tabrb[ri]], [tBb[ai]])
                    TT('dve', B3[:, :, 64:128], X[:, :, 0:64], ps_, ALU.mult, [pzb, tabrb[ri]], [tBb[ai]])
                    oi = nxt('ob16', 4)
                    if which == 0:
                        TT('pool', ob16[oi][:n, :], tA[ai][:n, :], tB[ai][:n, :], ALU.add, [tAb[ai], tBb[ai]], [ob16b[oi]])
                        DMA('sp', qr_d[t0:t0 + n, :], ob16[oi][:n, :], [ob16b[oi]], [scrb[t]])
                    else:
                        TT('pool', tA[ai][:n, :], tA[ai][:n, :], tB[ai][:n, :], ALU.add, [tBb[ai]], [tAb[ai]])
                        kd_ = (kdt if t < 16 else kdts)[:n, :].unsqueeze(2).to_broadcast([n, 4, 128])
                        TT('pool', ob16[oi][:n, :].rearrange("p (h d) -> p h d", h=4), A3, kd_, ALU.mult, [tAb[ai], kdtb], [ob16b[oi]])
                        DMA('sp', kd_d[t0:t0 + n, :], ob16[oi][:n, :], [ob16b[oi]], [scrb[t]])
            Rw, rwb = ring_next()
            wv = Rw[:, 0:8 * 1024].rearrange("p (k f) -> p k f", k=8)
            DMA('pool', wv, W["w_in"].rearrange("(k p) f -> p k f", p=128)[:, :, 1440:2464], [], [rwb])
            for t in range(NT):
                n = tsz(t)
                t0 = 128 * t
                for which in range(2):
                    pz, pzb = getA()
                    for k in range(8):
                        mm(pz[:n, :], uT[:, k, t0:t0 + n], wv[:, k, which * 512:(which + 1) * 512], k == 0, k == 7, [uTb[t], rwb], [pzb])
                    oi = nxt('ob16', 4)
                    if which == 0:
                        CP('act', ob16[oi][:n, :], pz[:n, :], [pzb], [ob16b[oi]])
                        DMA('sp', v_d[t0:t0 + n, :], ob16[oi][:n, :], [ob16b[oi]], [scrb[t]])
                    else:
                        ACT(ob16[oi][:n, :], pz[:n, :], AF.Silu, [pzb], [ob16b[oi]])
                        DMA('sp', rg_d[t0:t0 + n, :], ob16[oi][:n, :], [ob16b[oi]], [scrb[t]])
            barrier()

            o = O_H
            qdtab = V32(o, 512); o += 2 * KB
            dtab = V32(o, 512); o += 2 * KB
            eyeT = V32(o, 272); o += 2 * KB
            rtabb = Buf()
            DMA('sp', qdtab, qdtab_d, [], [rtabb])
            DMA('sp', dtab, dtab_d, [], [rtabb])
            DMA('sp', eyeT, eyeT_d, [], [rtabb])
            ld = []
            for i in range(3):
                ld.append(dict(q=V16(o, 512), k=V16(o + KB, 512), v=V16(o + 2 * KB, 512), g=V16(o + 3 * KB, 512), b=Buf()))
                o += 4 * KB
            Sacc = [V32(o + 2 * KB * j, 512) for j in range(4)]; o += 8 * KB
            Saccb = [Buf() for _ in range(4)]
            Tm = V32(o, 512); o += 2 * KB
            Tmb = Buf()
            Sin = [V32(o, 512), V32(o + 2 * KB, 512)]; o += 4 * KB
            Sinb = [Buf(), Buf()]
            Sst = V32(o, 512); o += 2 * KB
            Sbf = V16(o, 512); o += KB
            Sstb = Buf()
            tmpS = V32(o, 512); o += 2 * KB
            tmpSb = Buf()
            QT = V16(o, 512); o += KB
            QdT = V16(o, 512); o += KB
            KdT = V16(o, 512); o += KB
            qkb = Buf()
            sTm = V16(o, 512); o += KB
            sTmb = Buf()
            o_sb = V32(o, 512); o += 2 * KB
            d_sb = V32(o, 512); o += 2 * KB
            sq_sb = V32(o, 512); o += 2 * KB
            osbb = Buf()
            y16 = V16(o, 512); o += KB
            y16b = Buf()
            QmT = V16(o, 4 * 256); o += 2 * KB
            QmTb = Buf()
            Km = V16(o, 512); o += KB
            Kmb = Buf()
            inner = V32(o, 512); o += 2 * KB
            innerb = Buf()

            def load_tile(t, what):
                n = tsz(t)
                t0 = 128 * t
                L = ld[nxt('ld', 3)]
                for key, src in (('q', qr_d), ('k', kd_d), ('v', v_d), ('g', rg_d)):
                    if key in what:
                        DMA('sp', L[key][:n, :], src[t0:t0 + n, :], [scrb[t]], [L['b']])
                return L

            def h3(ap, n):
                return ap[:n, :].rearrange("p (h d) -> p h d", h=4)

            def bc4(col0, n=128):
                return gtab[:n, col0:col0 + 4].unsqueeze(2).to_broadcast([n, 4, 128])

            pre1 = {0: load_tile(0, 'kv')}
            for j in range(4):
                for c in range(4):
                    t = 4 * j + c
                    L = pre1.pop(t)
                    if t + 1 < 16:
                        pre1[t + 1] = load_tile(t + 1, 'kv')
                    pU, pUb = getA()
                    for hh in range(4):
                        mm(pU[:, hh * 128:(hh + 1) * 128], L['k'][:, hh * 128:(hh + 1) * 128], L['v'][:, hh * 128:(hh + 1) * 128],
                           True, True, [L['b']], [pUb])
                    cf = bc4(12 + 4 * c)
                    if c == 0:
                        TT('dve', h3(Sacc[j], 128), h3(pU, 128), cf, ALU.mult, [pUb, gtabb], [Saccb[j]])
                    else:
                        TT('dve', h3(tmpS, 128), h3(pU, 128), cf, ALU.mult, [pUb, gtabb], [tmpSb])
                        TT('dve', Sacc[j], Sacc[j], tmpS, ALU.add, [tmpSb], [Saccb[j]])
                DMA('sp', cc2_in[128 * j:128 * j + 128, :], Sacc[j], [Saccb[j]], [cc2b])
            S.cc(lambda e: e.collective_compute("AllGather", ALU.bypass, replica_groups=[[0, 1, 2, 3], [4, 5, 6, 7]],
                                                ins=[cc2_in], outs=[cc2_out]), r=[cc2b], w=[cc2ob])
            MEMSET('dve', Tm, 0.0, [Tmb])
            for j in range(4):
                MEMSET('dve', Sacc[j], 0.0, [Saccb[j]])
            for m in range(16):
                jm, rm = m // 4, m % 4
                STT('dve', Sacc[jm], Tm, gtab[:, 28 + rm:29 + rm], Sacc[jm], ALU.mult, ALU.add, [Tmb, gtabb], [Saccb[jm]])
                if m < 15:
                    si = nxt('Sin', 2)
                    row = (rm * 4 + jm) * 128
                    DMA('sp', Sin[si], cc2_out[row:row + 128, :], [cc2ob], [Sinb[si]])
                    TT('dve', h3(Tm, 128), h3(Tm, 128), bc4(4), ALU.mult, [gtabb], [Tmb])
                    TT('dve', Tm, Tm, Sin[si], ALU.add, [Sinb[si]], [Tmb])

            def ret_out(L, n, t0, pOo, pOob, extra_r):
                CP('act', o_sb[:n, :], pOo[:n, :], [pOob] + extra_r, [osbb])
                RED('dve', st4[:n, 0:4], h3(o_sb, n), [osbb], [osbb])
                TS('dve', st4[:n, 4:8], st4[:n, 0:4], -1.0 / 128, None, ALU.mult, None, [osbb], [osbb])
                TT('dve', h3(d_sb, n), h3(o_sb, n), st4[:n, 4:8].unsqueeze(2).to_broadcast([n, 4, 128]), ALU.add, [osbb], [osbb])
                ACT(sq_sb[:n, :], d_sb[:n, :], AF.Square, [osbb], [osbb])
                RED('dve', st4[:n, 8:12], h3(sq_sb, n), [osbb], [osbb])
                TS('dve', st4[:n, 12:16], st4[:n, 8:12], 1.0 / 128, EPS, ALU.mult, ALU.add, [osbb], [osbb])
                ACT(st4[:n, 12:16], st4[:n, 12:16], AF.Sqrt, [osbb], [osbb])
                S.op('dve', lambda e: e.reciprocal(out=st4[:n, 12:16], in_=st4[:n, 12:16]), r=[osbb], w=[osbb])
                TT('dve', h3(d_sb, n), h3(d_sb, n), st4[:n, 12:16].unsqueeze(2).to_broadcast([n, 4, 128]), ALU.mult, [osbb], [osbb])
                TT('dve', y16[:n, :], d_sb[:n, :], L['g'][:n, :], ALU.mult, [osbb, L['b']], [y16b])
                p, pb = getT()
                for hh in range(4):
                    tr(p[:, hh, 0:n], y16[:n, hh * 128:(hh + 1) * 128], n, [y16b], [pb], hh == 3)
                for hh in range(4):
                    AMUL(o_retT[:, hh, t0:t0 + n], p[:, hh, 0:n], gr_col[:, hh:hh + 1], [pb, grb], [oretb])

            pre2 = {0: load_tile(0, 'qkvg')}
            for j in range(4):
                CP('act', Sst, Sacc[j], [Saccb[j]], [Sstb])
                CP('dve', Sbf, Sacc[j], [Saccb[j]], [Sstb])
                for c in range(4):
                    t = 4 * j + c
                    t0 = 128 * t
                    L = pre2.pop(t)
                    pre2[t + 1] = load_tile(t + 1, 'qkvg')
                    p, pb = getT()
                    for hh in range(4):
                        tr(p[:, hh, :], L['q'][:, hh * 128:(hh + 1) * 128], 128, [L['b']], [pb], False)
                    for hh in range(4):
                        tr(p[:, 4 + hh, :], L['k'][:, hh * 128:(hh + 1) * 128], 128, [L['b']], [pb], hh == 3)
                    CP('act', QT.rearrange("p (h d) -> p h d", h=4), p[:, 0:4, :], [pb], [qkb])
                    TT('dve', QdT.rearrange("p (h d) -> p h d", h=4), p[:, 0:4, :], qdtab.rearrange("p (h d) -> p h d", h=4), ALU.mult,
                       [pb, rtabb], [qkb])
                    CP('act', KdT.rearrange("p (h d) -> p h d", h=4), p[:, 4:8, :], [pb], [qkb])
                    pS, pSb = getA()
                    for hh in range(4):
                        mm(pS[:, hh * 128:(hh + 1) * 128], KdT[:, hh * 128:(hh + 1) * 128], QT[:, hh * 128:(hh + 1) * 128],
                           True, True, [qkb], [pSb])
                    TT('dve', sTm, pS, dtab, ALU.mult, [pSb, rtabb], [sTmb])
                    pOo, pOob = getA()
                    for hh in range(4):
                        sl = slice(hh * 128, (hh + 1) * 128)
                        mm(pOo[:, sl], sTm[:, sl], L['v'][:, sl], True, False, [sTmb, L['b']], [pOob])
                        mm(pOo[:, sl], QdT[:, sl], Sbf[:, sl], False, True, [qkb, Sstb], [pOob])
                    pU, pUb = getA()
                    for hh in range(4):
                        sl = slice(hh * 128, (hh + 1) * 128)
                        mm(pU[:, sl], L['k'][:, sl], L['v'][:, sl], True, True, [L['b']], [pUb])
                    TT('dve', h3(Sst, 128), h3(Sst, 128), bc4(0), ALU.mult, [gtabb], [Sstb])
                    TT('dve', Sst, Sst, pU, ALU.add, [pUb], [Sstb])
                    CP('act', Sbf, Sst, [], [Sstb])
                    ret_out(L, 128, t0, pOo, pOob, [])
            DMA('sp', retp_o.rearrange("h k v -> k h v"), Sst.rearrange("p (h d) -> p h d", h=4), [Sstb], [outb])

            L = pre2.pop(16)
            p, pb = getT()
            for hh in range(4):
                tr(p[:, hh, 0:NS], L['q'][:NS, hh * 128:(hh + 1) * 128], NS, [L['b']], [pb], hh == 3)
            TT('dve', QmT.rearrange("p (h b t) -> p h b t", h=4, b=NS),
               p[:, 0:4, 0:NS].unsqueeze(2).to_broadcast([128, 4, NS, NS]),
               eyeT[:, 0:256].rearrange("p (b t) -> p b t", b=NS).unsqueeze(1).to_broadcast([128, 4, NS, NS]), ALU.mult, [pb, rtabb], [QmTb])
            TT('dve', tmpS[:NS, :], L['q'][:NS, :], L['k'][:NS, :], ALU.mult, [L['b']], [tmpSb])
            RED('dve', st4[:NS, 16:20], h3(tmpS, NS), [tmpSb], [tmpSb])
            TT('dve', h3(inner, NS), h3(L['v'], NS), st4[:NS, 16:20].unsqueeze(2).to_broadcast([NS, 4, 128]), ALU.mult,
               [tmpSb, L['b']], [innerb])
            pC, pCb = getO()
            QmT4 = QmT.rearrange("p (h b t) -> p h b t", h=4, b=NS)
            def ld_state(b):
                si = nxt('Sin', 2)
                DMA('sp', Sin[si].rearrange("p (h d) -> p h d", h=4), state_d[b].rearrange("h k v -> k h v"), [], [Sinb[si]])
                return si
            sis = {0: ld_state(0)}
            for b in range(NS):
                si = sis.pop(b)
                if b + 1 < NS:
                    sis[b + 1] = ld_state(b + 1)
                CP('act', Sbf, Sin[si], [Sinb[si]], [Sstb])
                for hh in range(4):
                    sl = slice(hh * 128, (hh + 1) * 128)
                    mm(pC[:NS, sl], QmT4[:, hh, b, :], Sbf[:, sl], b == 0 and hh == 0, b == NS - 1, [QmTb, Sstb], [pCb], sgc=True)
                TS('dve', Km[:NS, :], L['k'][:NS, :], eyeT[:NS, 256 + b:257 + b], None, ALU.mult, None, [L['b'], rtabb], [Kmb])
                pU, pUb = getA()
                for hh in range(4):
                    sl = slice(hh * 128, (hh + 1) * 128)
                    mm(pU[:, sl], Km[:NS, sl], L['v'][:NS, sl], True, True, [Kmb, L['b']], [pUb])
                TT('dve', h3(Sst, 128), h3(Sin[si], 128), bc4(8), ALU.mult, [Sinb[si], gtabb], [Sstb])
                TT('dve', Sst, Sst, pU, ALU.add, [pUb], [Sstb])
                DMA('sp', rets_o[b].rearrange("h k v -> k h v"), Sst.rearrange("p (h d) -> p h d", h=4), [Sstb], [outb])
            TT('dve', h3(tmpS, NS), h3(pC, NS), bc4(8, NS), ALU.mult, [pCb, gtabb], [tmpSb])
            TT('dve', inner[:NS, :], inner[:NS, :], tmpS[:NS, :], ALU.add, [tmpSb], [innerb])
            ret_out(L, NS, TOKP, inner, innerb, [])
            barrier()


            if KSTOP == 'ret':
                raise _Stop()
            QTa = V16(O_UT, 8 * TOK).rearrange("p (h t) -> p h t", h=8)
            QTb = Buf()
            RB = O_RING
            KTb_ = [V16(RB, 8192).rearrange("p (j r k) -> p j r k", j=4, r=4), V16(RB + 16 * KB, 8192).rearrange("p (j r k) -> p j r k", j=4, r=4)]
            KTbb = [Buf(), Buf()]
            Pb_ = [V16(RB + 32 * KB + KB * i, 512) for i in range(3)]
            Pbb = [Buf() for _ in range(3)]
            wkvb = V16(RB + 36 * KB, 1024); wkvbb = Buf()
            wqb = V16(RB + 38 * KB, 2 * 768).rearrange("p (k f) -> p k f", k=2); wqbb = Buf()
            rl = V32(RB + 42 * KB, 8)
            o = O_H
            ckvT = V16(o, 8192).rearrange("p (r j k) -> p r j k", r=4, j=4); o += 16 * KB
            ckvTb = Buf()
            Vb_ = [V16(o, 64 * 65).rearrange("p (t d) -> p t d", t=64), V16(o + 8320, 64 * 65).rearrange("p (t d) -> p t d", t=64)]
            o += 2 * 8320
            Vbb = [Buf(), Buf()]
            o_att = V16(o, 16 * 512).rearrange("p (t d) -> p t d", t=16); o += 16 * KB
            o_attb = Buf()
            amask = V16(o, 16 * 512).rearrange("p (m q) -> p m q", m=16); o += 16 * KB
            amaskb = Buf()
            o = O_SPARE
            qs = [V16(o, 768).rearrange("p (h d) -> p h d", h=8), V16(o + 1536, 768).rearrange("p (h d) -> p h d", h=8)]; o += 3072
            qsb = [Buf(), Buf()]
            tqa = V32(o, 256); o += KB
            tqb = V32(o, 256); o += KB
            tqab = Buf()

            DMA('pool', wqb, W["w_q_b"].rearrange("(k p) f -> p k f", p=128), [], [wqbb])
            DMA('pool', wkvb, W["w_kv_b"], [], [wkvbb])
            for m4 in range(4):
                DMA('sp', amask[:, 4 * m4:4 * m4 + 4, :], amask_d[4 * m4:4 * m4 + 4].rearrange("m p q -> p m q"), [], [amaskb])
            for r_ in range(4):
                DMA('sp', ckvT[:, r_, :, :], cc1_out[r_ * 160:r_ * 160 + 128, :].rearrange("p (j k) -> p j k", j=4), [cc1ob], [ckvTb])
            for i in range(2):
                MEMSET('dve', Vb_[i][:, :, 64:65], 1.0, [Vbb[i]])
            MEMSET('dve', o_attT[:, :, TOKP:TOK], 0.0, [oattb])

            if KSTOP == 'qsetup':
                raise _Stop()
            QSTEP = int(os.environ.get('QSTEP', '9'))
            for t in range(NT):
                n = tsz(t)
                t0 = 128 * t
                qi = nxt('qs', 2)
                for (h0, nh) in ((0, 5), (5, 3)):
                    pz, pzb = getA()
                    for k in range(2):
                        mm(pz[:n, 0:nh * 96], cqT[:, k, t0:t0 + n], wqb[:, k, h0 * 96:(h0 + nh) * 96], k == 0, k == 1, [cqTb[t], wqbb], [pzb])
                    Xv = pz[:n, 0:nh * 96].rearrange("p (h d) -> p h d", h=nh)
                    if QSTEP >= 2:
                        CP('act', qs[qi][:n, h0:h0 + nh, 0:64], Xv[:, :, 0:64], [pzb], [qsb[qi]])
                    A3 = tqa[:n, 0:nh * 32].rearrange("p (h d) -> p h d", h=nh)
                    B3 = tqb[:n, 0:nh * 32].rearrange("p (h d) -> p h d", h=nh)
                    if QSTEP >= 3:
                        TT('dve', A3, Xv[:, :, 64:96], tabm[:n, t, 0:32].unsqueeze(1).to_broadcast([n, nh, 32]), ALU.mult, [pzb, tabmb], [tqab])
                        TT('dve', B3[:, :, 0:16], Xv[:, :, 80:96], tabm[:n, t, 32:48].unsqueeze(1).to_broadcast([n, nh, 16]), ALU.mult,
                           [pzb, tabmb], [tqab])
                        TT('dve', B3[:, :, 16:32], Xv[:, :, 64:80], tabm[:n, t, 48:64].unsqueeze(1).to_broadcast([n, nh, 16]), ALU.mult,
                           [pzb, tabmb], [tqab])
                        TT('dve', qs[qi][:n, h0:h0 + nh, 64:96], A3, B3, ALU.add, [tqab], [qsb[qi]])
                if QSTEP >= 4:
                    p, pb = getT()
                    for hh in range(8):
                        tr(p[0:96, hh, 0:n], qs[qi][:n, hh, :], n, [qsb[qi]], [pb], hh == 7)
                    if QSTEP >= 5:
                        CP('act', QTa[0:96, :, t0:t0 + n], p[0:96, :, 0:n], [pb], [QTb])

            if KSTOP == 'qproj':
                raise _Stop()
            SCALE = float(96.0 ** -0.5)
            evk = [0]
            LOOK = 3

            def build_kv(hh):
                KT, KTb = KTb_[hh % 2], KTbb[hh % 2]
                Vv, Vvb = Vb_[hh % 2], Vbb[hh % 2]
                for r_ in range(4):
                    DMA('sp', KT[64:96, :, r_, :], cc1_out[r_ * 160 + 128:r_ * 160 + 160, :].rearrange("p (j k) -> p j k", j=4), [cc1ob], [KTb])
                for g in range(16):
                    j_, r_ = g // 4, g % 4
                    pk, pkb = getA()
                    mm(pk[0:64, :], wkvb[:, hh * 128:hh * 128 + 64], ckvT[:, r_, j_, :], True, True, [wkvbb, ckvTb], [pkb])
                    CP('act' if g % 2 == 0 else 'dve', KT[0:64, j_, r_, :], pk[0:64, :], [pkb], [KTb])
                for k8 in range(8):
                    pv, pvb = getA()
                    for i in range(8):
                        kt = k8 * 8 + i
                        g, kk = kt // 4, kt % 4
                        j_, r_ = g // 4, g % 4
                        mm(pv[:, i * 64:(i + 1) * 64], ckvT[:, r_, j_, kk * 128:(kk + 1) * 128], wkvb[:, hh * 128 + 64:hh * 128 + 128],
                           True, True, [wkvbb, ckvTb], [pvb])
                    CP('dve', Vv[:, k8 * 8:(k8 + 1) * 8, 0:64], pv.rearrange("p (i d) -> p i d", i=8), [pvb], [Vvb])

            def att_head(hh):
                KT, KTb = KTb_[hh % 2], KTbb[hh % 2]
                Vv, Vvb = Vb_[hh % 2], Vbb[hh % 2]
                units = [(j, kt, 16 * j + 16) for j in range(4) for kt in range(16 * j + 16)]
                pos = {}
                pend = []

                def score(u):
                    j, kt, nk = u
                    if kt == 0:
                        pos[j] = getO()
                    g, kk = kt // 4, kt % 4
                    j_, r_ = g // 4, g % 4
                    ps_, psb = getA()
                    mm(ps_, KT[0:96, j_, r_, kk * 128:(kk + 1) * 128], QTa[0:96, hh, j * 512:(j + 1) * 512], True, True, [KTb, QTb], [psb])
                    return ps_, psb

                def rest(u, ps_, psb):
                    j, kt, nk = u
                    po, pob = pos[j]
                    g, kk = kt // 4, kt % 4
                    pi = nxt('P', 3)
                    ACT(Pb_[pi], ps_, AF.Exp, [psb], [Pbb[pi]], scale=SCALE)
                    if g >= 4 * j:
                        TT('pool', Pb_[pi], Pb_[pi], amask[:, (g - 4 * j) * 4 + kk, :], ALU.mult, [amaskb], [Pbb[pi]])
                    for qt in range(4):
                        mm(po[:, qt * 68:qt * 68 + 65], Pb_[pi][:, qt * 128:(qt + 1) * 128], Vv[:, kt, :], kt == 0 and qt == 0,
                           kt == nk - 1, [Pbb[pi], Vvb], [pob], sgc=True)
                    if kt == nk - 1:
                        po3 = po[:, 0:272].rearrange("p (q d) -> p q d", q=4)
                        S.op('dve', lambda e, po3=po3: e.reciprocal(out=rl[:, 0:4], in_=po3[:, :, 64]), r=[pob], w=[o_attb])
                        TT('dve', o_att[:, 4 * j:4 * j + 4, hh * 64:(hh + 1) * 64], po3[:, :, 0:64],
                           rl[:, 0:4].unsqueeze(2).to_broadcast([128, 4, 64]), ALU.mult, [pob], [o_attb])

                for u in units:
                    pend.append((u,) + score(u))
                    if len(pend) >= LOOK:
                        rest(*pend.pop(0))
                while pend:
                    rest(*pend.pop(0))

            build_kv(0)
            for hh in range(8):
                if hh + 1 < 8:
                    build_kv(hh + 1)
                att_head(hh)
            for t in range(16):
                p, pb = getT()
                for c in range(4):
                    tr(p[:, c, :], o_att[:, t, c * 128:(c + 1) * 128], 128, [o_attb], [pb], c == 3)
                CP('act', o_attT[:, :, 128 * t:128 * t + 128], p[:, 0:4, :], [pb], [oattb])
            barrier()

            if KSTOP == 'attn':
                raise _Stop()

            if with_cache:
                o = O_H
                Gc = [V16(o + 4 * KB * i, 2048).rearrange("p (t c) -> p t c", t=16) for i in range(3)]; o += 12 * KB
                Gcb = [Buf() for _ in range(3)]
                Gk = [V16(o + 8 * KB * i, 4096) for i in range(2)]; o += 16 * KB
                Gkb = [Buf(), Buf()]
                kT5 = [V16(o + 1280 * i, 640).rearrange("p (s k) -> p s k", s=5) for i in range(2)]; o += 2560
                kT5b = [Buf(), Buf()]
                Ps = [V16(o + 64 * i, 32) for i in range(3)]; o += 256
                Psb = [Buf() for _ in range(3)]
                Lacc = V32(o, 32); o += 128
                Laccb = Buf()
                Lred = V32(o, 8); o += 32
                ones1 = V32(o, 8); o += 32
                rl8 = V32(o, 8); o += 32
                ptsb = V32(o, 16, I32); o += 64
                idx8 = V32(o, 16, I32); o += 64
                idxb = Buf()
                wukT = V16(o, 1024).rearrange("p (h c) -> p h c", h=8); o += 2048
                wuv = V16(o, 512).rearrange("p (h d) -> p h d", h=8); o += 1024
                qlatT = V16(o, 128).rearrange("p (h b) -> p h b", h=8); o += 256
                Qblk = V16(o, 512).rearrange("p (g b) -> p g b", b=NS); o += 1024
                olatT = V16(o, 128).rearrange("p (h b) -> p h b", h=8); o += 256
                olat = V16(o, 128); o += 256
                P2 = V16(o, 8); o += 32
                P2m = V16(o, 8); o += 32
                sconb = Buf()
                DMA('sp', ptsb, ptT_d, [], [idxb])
                DMA('sp', eyeS[:NS, :], eyeT_d[0:NS, 256:272], [], [idxb])
                S.op('dve', lambda e: e.tensor_single_scalar(out=idx8, in_=ptsb, scalar=3, op=ALU.logical_shift_left), r=[idxb], w=[idxb])
                MEMSET('dve', ones1, 1.0, [sconb])
                MEMSET('dve', Qblk, 0.0, [sconb])
                p, pb = getT()
                for hh in range(8):
                    tr(p[0:64, hh, :], wkvb[:, hh * 128:hh * 128 + 64], 128, [wkvbb], [pb], hh == 7)
                CP('act', wukT[0:64, :, :], p[0:64, :, :], [pb], [sconb])
                CP('dve', wuv, wkvb.rearrange("p (h d) -> p h d", h=8)[:, :, 64:128], [wkvbb], [sconb])
                pq, pqb = getA()
                for hh in range(8):
                    mm(pq[:, hh * NS:(hh + 1) * NS], wukT[0:64, hh, :], QTa[0:64, hh, TOKP:TOK], True, True, [sconb, QTb], [pqb])
                CP('act', qlatT, pq[:, 0:8 * NS].rearrange("p (h b) -> p h b", h=8), [pqb], [sconb])
                for g in range(4):
                    DMA('sp', Qblk[32 * g:32 * g + 32, 8 * g:8 * g + 8, :], QTa[64:96, :, TOKP:TOK], [QTb, sconb], [sconb])
                def issue_gk(b):
                    gk, gkb = Gk[b % 2], Gkb[b % 2]
                    S.dma('pool', lambda e, gk=gk, b=b: e.indirect_dma_start(
                        out=gk, out_offset=None, in_=ckpe_d[:, :],
                        in_offset=bass.IndirectOffsetOnAxis(ap=ptsb[:, b:b + 1], axis=0)), r=[idxb], w=[gkb])

                def issue_gc(n):
                    b, ch = n // 8, n % 8
                    gc, gcb = Gc[n % 3], Gcb[n % 3]
                    S.dma('pool', lambda e, gc=gc, b=b, ch=ch: e.indirect_dma_start(
                        out=gc.rearrange("p t c -> p (t c)"), out_offset=None, in_=cckv_d[:, 0:2048], element_offset=ch * 2048,
                        in_offset=bass.IndirectOffsetOnAxis(ap=idx8[:, b:b + 1], axis=0)), r=[idxb], w=[gcb])

                issue_gk(0)
                issue_gc(0)
                issue_gc(1)
                for b in range(NS):
                    MEMSET('dve', Lacc, 0.0, [Laccb])
                    gk, gkb = Gk[b % 2], Gkb[b % 2]
                    if b + 1 < NS:
                        issue_gk(b + 1)
                    gk3 = gk.rearrange("p (t r) -> p t r", t=32)
                    pol, polb = getO()
                    st_ = {}

                    def stageA(g):
                        ch, tg = g // 4, g % 4
                        n = b * 8 + ch
                        if tg == 2 and n + 2 < NS * 8:
                            issue_gc(n + 2)
                        gc, gcb = Gc[n % 3], Gcb[n % 3]
                        p, pb = getT()
                        for i in range(4):
                            tr(p[:, i, :], gc[:, tg * 4 + i, :], 128, [gcb], [pb], False)
                        tr(p[:, 4, :], gk3[:, g, :], 128, [gkb], [pb], True)
                        ki = nxt('kT5', 2)
                        k5, k5b = kT5[ki], kT5b[ki]
                        CP('act' if g % 2 == 0 else 'dve', k5, p[:, 0:5, :], [pb], [k5b])
                        st_[g] = dict(gc=gc, gcb=gcb, k5=k5, k5b=k5b, tg=tg)

                    def stageB(g):
                        d = st_[g]
                        k5, k5b = d['k5'], d['k5b']
                        ps_, psb = getA()
                        mm(ps_[:, 0:32], k5[:, 4, :], Qblk[:, :, b], True, False, [k5b, sconb], [psb], sgc=True)
                        for i in range(4):
                            mm(ps_[:, i * 8:(i + 1) * 8], k5[:, i, :], qlatT[:, :, b], False, i == 3, [k5b, sconb], [psb], sgc=True)
                        pi = nxt('Ps', 3)
                        ACT(Ps[pi], ps_[:, 0:32], AF.Exp, [psb], [Psb[pi]], scale=SCALE)
                        TT('dve', Lacc, Lacc, Ps[pi], ALU.add, [Psb[pi]], [Laccb])
                        d['pi'] = pi

                    def stageC(g):
                        d = st_[g]
                        pi, gc, gcb, tg = d['pi'], d['gc'], d['gcb'], d['tg']
                        for i in range(4):
                            mm(pol[0:8, 0:128], Ps[pi][:, i * 8:(i + 1) * 8], gc[:, tg * 4 + i, :],
                               g == 0 and i == 0, False, [Psb[pi], gcb], [polb], sgc=True)

                    NG = 32
                    for step in range(NG + 2):
                        if step < NG:
                            stageA(step)
                        if 0 <= step - 1 < NG:
                            stageB(step - 1)
                        if 0 <= step - 2 < NG:
                            stageC(step - 2)
                    ps2, ps2b = getA()
                    mm(ps2[0:NS, 0:8], ckvS_T[:, 0:NS], qlatT[:, :, b], True, False, [smpb, sconb], [ps2b], sgc=True)
                    mm(ps2[0:NS, 0:8], kpeS_T[0:32, 0:NS], Qblk[0:32, 0:8, b], False, True, [smpb, sconb], [ps2b], sgc=True)
                    ACT(P2[:NS, :], ps2[0:NS, 0:8], AF.Exp, [ps2b], [sconb], scale=SCALE)
                    TS('dve', P2m[:NS, :], P2[:NS, :], eyeS[:NS, b:b + 1], None, ALU.mult, None, [sconb, idxb], [sconb])
                    TT('dve', Lacc[:NS, 0:8], Lacc[:NS, 0:8], P2m[:NS, :], ALU.add, [sconb], [Laccb])
                    mm(pol[0:8, 0:128], P2m[:NS, :], ckvS_tok[:NS, :], False, True, [sconb, smpb], [polb], sgc=True)
                    RED('dve', Lred, Lacc.rearrange("p (t h) -> p h t", t=4), [Laccb], [Laccb])
                    pl, plb = getA()
                    mm(pl[0:8, 0:1], Lred, ones1[:, 0:1], True, True, [Laccb, sconb], [plb])
                    S.op('dve', lambda e, pl=pl: e.reciprocal(out=rl8[0:8, 0:1], in_=pl[0:8, 0:1]), r=[plb], w=[sconb])
                    TS('dve', olat[0:8, :], pol[0:8, 0:128], rl8[0:8, 0:1], None, ALU.mult, None, [polb, sconb], [sconb])
                    p, pb = getT()
                    tr(p[:, 0, 0:8], olat[0:8, :], 8, [sconb], [pb], True)
                    CP('act', olatT[:, :, b], p[:, 0, 0:8], [pb], [sconb])
                for hp in range(4):
                    pf, pfb = getA()
                    mm(pf[:, 0:2 * NS], wuv[:, 2 * hp:2 * hp + 2, :].rearrange("p h d -> p (h d)"),
                       olatT[:, 2 * hp:2 * hp + 2, :].rearrange("p h b -> p (h b)"), True, True, [sconb], [pfb])
                    CP('act', o_attT[0:64, hp, TOKP:TOK], pf[0:64, 0:NS], [pfb], [oattb])
                    CP('act', o_attT[64:128, hp, TOKP:TOK], pf[64:128, NS:2 * NS], [pfb], [oattb])
                barrier()
            if KSTOP == 'samp':
                raise _Stop()
            mix_norm_all(True)
            wga = V16(RB, 8192).rearrange("p (k f) -> p k f", k=8)
            wgr = V16(RB + 16 * KB, 8192).rearrange("p (k f) -> p k f", k=8)
            wba = V16(RB + 32 * KB, 4096).rearrange("p (k f) -> p k f", k=4)
            wbr = V16(RB + 40 * KB, 4096).rearrange("p (k f) -> p k f", k=4)
            mwb = Buf()
            DMA('pool', wga, W["w_in"].rearrange("(k p) f -> p k f", p=128)[:, :, 2464:3488], [], [mwb])
            DMA('pool', wgr, W["w_in"].rearrange("(k p) f -> p k f", p=128)[:, :, 3488:4512], [], [mwb])
            DMA('pool', wba, W["w_branch_att"].rearrange("(k p) f -> p k f", p=128), [], [mwb])
            DMA('pool', wbr, W["w_branch_ret"].rearrange("(k p) f -> p k f", p=128), [], [mwb])
            o = O_H
            mergedT = V16(o, 8 * TOK).rearrange("p (k t) -> p k t", k=8); o += 33 * KB
            mergedb = [Buf() for _ in range(5)]
            sgA = [V32(o, 512), V32(o + 2 * KB, 512)]; o += 4 * KB
            sgAb = [Buf(), Buf()]
            sgR = [V32(o, 512), V32(o + 2 * KB, 512)]; o += 4 * KB
            sgRb = [Buf(), Buf()]
            hst2 = [V32(o, D), V32(o + 4 * KB, D)]; o += 8 * KB
            hst2b = [Buf(), Buf()]
            for b in range(5):
                bt0, bn = blk_tok(b)
                tiles = list(range(blocks[b][0], blocks[b][0] + blocks[b][1]))
                ub = [uTb[t] for t in tiles]
                for f in range(8):
                    fs = slice(f * 128, (f + 1) * 128)
                    pga, pgab = getA()
                    pgr, pgrb = getA()
                    pa, pab = getA()
                    pr, prb = getA()
                    for k in range(8):
                        mm(pga[:, 0:bn], wga[:, k, fs], uT[:, k, bt0:bt0 + bn], k == 0, k == 7, ub + [mwb], [pgab])
                    for k in range(8):
                        mm(pgr[:, 0:bn], wgr[:, k, fs], uT[:, k, bt0:bt0 + bn], k == 0, k == 7, ub + [mwb], [pgrb])
                    for k in range(4):
                        mm(pa[:, 0:bn], wba[:, k, fs], o_attT[:, k, bt0:bt0 + bn], k == 0, k == 3, [oattb, mwb], [pab])
                    for k in range(4):
                        mm(pr[:, 0:bn], wbr[:, k, fs], o_retT[:, k, bt0:bt0 + bn], k == 0, k == 3, [oretb, mwb], [prb])
                    si = nxt('sgA', 2)
                    ACT(sgA[si][:, 0:bn], pga[:, 0:bn], AF.Sigmoid, [pgab], [sgAb[si]])
                    ACT(sgR[si][:, 0:bn], pgr[:, 0:bn], AF.Sigmoid, [pgrb], [sgRb[si]])
                    TT('dve', sgA[si][:, 0:bn], sgA[si][:, 0:bn], pa[:, 0:bn], ALU.mult, [pab], [sgAb[si]])
                    TT('dve', sgR[si][:, 0:bn], sgR[si][:, 0:bn], pr[:, 0:bn], ALU.mult, [prb], [sgRb[si]])
                    TT('pool', mergedT[:, f, bt0:bt0 + bn], sgA[si][:, 0:bn], sgR[si][:, 0:bn], ALU.add, [sgAb[si], sgRb[si]], [mergedb[b]])
            barrier()
            wo = V16(RB, 8192).rearrange("p (k f) -> p k f", k=8)
            wob = Buf()
            DMA('pool', wo, W["w_out"].rearrange("(k p) f -> p k f", p=128), [], [wob])
            for t in range(NT):
                n = tsz(t)
                t0 = 128 * t
                b = min(t // 4, 4)
                k_ = nxt('hst2', 2)
                DMA('sp', hst2[k_][:n, :], h_d[t0:t0 + n, :], [hdb[t]], [hst2b[k_]])
                for oh in range(2):
                    po_, pob_ = getA()
                    for k in range(8):
                        mm(po_[:n, :], mergedT[:, k, t0:t0 + n], wo[:, k, oh * 512:(oh + 1) * 512], k == 0, k == 7, [mergedb[b], wob], [pob_])
                    hs = hst2[k_][:n, oh * 512:(oh + 1) * 512]
                    TT('dve', hs, hs, po_[:n, :], ALU.add, [pob_], [hst2b[k_]])
                DMA('sp', h_d[t0:t0 + n, :], hst2[k_][:n, :], [hst2b[k_]], [hdb[t]])
            barrier()

            if KSTOP == 'wout':
                raise _Stop()
            for t in range(NT):
                n = tsz(t)
                DMA('sp', h[t][:n, :], h_d[128 * t:128 * t + n, :], [hdb[t]], [hb[t]])
            ffn_phase(W["ffn2_w_gate"], W["ffn2_w_up"], W["ffn2_w_down"], "ffn2_norm", 4)

            if KSTOP == 'ffn2':
                raise _Stop()
            mix_norm_all(False, "ple_norm", 6)
            wpg = V16(RB, 8192).rearrange("p (k f) -> p k f", k=8)
            wpp = V16(RB + 16 * KB, 2048).rearrange("p (k f) -> p k f", k=2)
            wpb = Buf()
            DMA('pool', wpg, W["w_ple_gate"].rearrange("(k p) f -> p k f", p=128), [], [wpb])
            DMA('pool', wpp, W["w_ple_proj"].rearrange("(k p) f -> p k f", p=128), [], [wpb])
            o = O_MULTI
            gfin = V32(o, D); o += 4 * KB
            gfinb = Buf()
            load_bc(gfin, gfinb, W["final_norm"], D)
            pst = [V32(o, 256), V32(o + KB, 256)]; o += 2 * KB
            pstb = [Buf(), Buf()]
            pbf = [V16(o, 256), V16(o + 512, 256)]; o += KB
            pbfb = [Buf(), Buf()]
            peT = [V16(o, 256).rearrange("p (k t) -> p k t", k=2), V16(o + 512, 256).rearrange("p (k t) -> p k t", k=2)]; o += KB
            peTb = [Buf(), Buf()]
            sgP = [V32(o, 512), V32(o + 2 * KB, 512)]; o += 4 * KB
            sgPb = [Buf(), Buf()]
            junk2 = V16(o, D); o += 2 * KB
            for t in range(NT):
                n = tsz(t)
                t0 = 128 * t
                i = nxt('pst', 2)
                DMA('sp', pst[i][:n, :], pin[t0:t0 + n, :], [], [pstb[i]])
                CP('act', pbf[i][:n, :], pst[i][:n, :], [pstb[i]], [pbfb[i]])
                p, pb = getT()
                for c in range(2):
                    tr(p[:, c, 0:n], pbf[i][:n, c * 128:(c + 1) * 128], n, [pbfb[i]], [pb], c == 1)
                CP('act', peT[i][:, :, 0:n], p[:, 0:2, 0:n], [pb], [peTb[i]])
                for oh in range(2):
                    os_ = slice(oh * 512, (oh + 1) * 512)
                    pg, pgb = getA()
                    pp, ppb = getA()
                    for k in range(8):
                        mm(pg[:n, :], uT[:, k, t0:t0 + n], wpg[:, k, os_], k == 0, k == 7, [uTb[t], wpb], [pgb])
                    for k in range(2):
                        mm(pp[:n, :], peT[i][:, k, 0:n], wpp[:, k, os_], k == 0, k == 1, [peTb[i], wpb], [ppb])
                    si = nxt('sgP', 2)
                    ACT(sgP[si][:n, :], pg[:n, :], AF.Sigmoid, [pgb], [sgPb[si]])
                    TT('dve', sgP[si][:n, :], sgP[si][:n, :], pp[:n, :], ALU.mult, [ppb], [sgPb[si]])
                    TT('pool', h[t][:n, os_], h[t][:n, os_], sgP[si][:n, :], ALU.add, [sgPb[si]], [hb[t]])
                col = 7 * NT + t
                sumsq(junk2, h[t][:n, :], n, D, col, [hb[t]])
                rstd_ops(col, n, D)
                STT('dve', h[t][:n, :], h[t][:n, :], rs[:n, col:col + 1], gfin[:n, :], ALU.mult, ALU.mult, [ssB(col), gfinb], [hb[t]])
                DMA('sp', y_o[t0:t0 + n, :], h[t][:n, :], [hb[t]], [outb])

        except _Stop:
            pass
        S.finish()
        S.emit(nc)
    return nc


def _core_tokens(c):
    r = c % 4
    idx = np.concatenate([np.arange(512 * (4 * j + r), 512 * (4 * j + r) + 512) for j in range(4)])
    return c // 4, idx


def _rope_tab(pos, half):
    pos = pos.astype(np.float32)
    inv = (np.float32(10000.0) ** (-np.arange(half, dtype=np.float32) / np.float32(half))).astype(np.float32)
    ang = pos[:, None] * inv[None, :]
    c, s = np.cos(ang).astype(np.float32), np.sin(ang).astype(np.float32)
    return np.concatenate([c, c, -s, s], axis=1).astype(np.float32)


def _core_consts(c):
    r = c % 4
    b, idx = _core_tokens(c)
    pos = np.concatenate([idx, np.full(NS, 16384)])
    lg = np.array(LG, dtype=np.float64)
    i = np.arange(128, dtype=np.float64)
    sc = 128.0 ** -0.5
    kdt = np.concatenate([np.exp(lg[None, :] * (127.0 - i)[:, None]) * sc, np.full((16, 4), sc)], 0).astype(np.float32)
    qd = np.exp(lg[:, None] * (i + 1.0)[None, :])
    qdtab = np.broadcast_to(qd.reshape(1, 512), (128, 512)).astype(np.float32)
    jj = i[:, None, None]
    ii = i[None, None, :]
    dt = np.where(ii >= jj, np.exp(lg[None, :, None] * (ii - 127.0)), 0.0)
    dtab = dt.reshape(128, 512).astype(np.float32)
    g = np.zeros(32, dtype=np.float64)
    g[0:4] = np.exp(lg * 128)
    g[4:8] = np.exp(lg * 512)
    g[8:12] = np.exp(lg)
    for cc in range(4):
        g[12 + 4 * cc:16 + 4 * cc] = np.exp(lg * 128 * (3 - cc))
    g[28 + r] = 1.0
    gtab = np.broadcast_to(g[None, :], (128, 32)).astype(np.float32)
    eyeT = np.zeros((128, 272), np.float32)
    eyeT[:, 0:256] = np.eye(16, dtype=np.float32).reshape(1, 256)
    eyeT[:16, 256:272] = np.eye(16, dtype=np.float32)
    return dict(tabm=_rope_tab(pos, 16), tabr=_rope_tab(pos, 64), kdt=kdt, qdtab=qdtab, dtab=dtab, gtab=gtab, eyeT=eyeT)


_NC_CACHE = {}


def kernel(**inp):
    import os
    with_cache = os.environ.get('KNOCACHE', '') == ''
    key = with_cache
    if key not in _NC_CACHE:
        _NC_CACHE[key] = build(with_cache)
    nc = _NC_CACHE[key]
    ident = np.eye(128).astype(ml_dtypes.bfloat16)
    wnames = ["ffn1_norm", "ffn1_w_gate", "ffn1_w_up", "ffn1_w_down", "mix_norm", "w_in", "q_a_norm", "w_q_b", "kv_a_norm",
              "w_kv_b", "ret_norm", "w_branch_att", "w_branch_ret", "w_out", "ffn2_norm", "ffn2_w_gate", "ffn2_w_up",
              "ffn2_w_down", "ple_norm", "w_ple_gate", "w_ple_proj", "final_norm"]
    shared = {}
    for nm in wnames:
        a = np.asarray(inp[nm], dtype=np.float32)
        if nm == "final_norm":
            a = a.reshape(1, D)
        elif nm == "ret_norm":
            a = a.reshape(1, 512)
        else:
            a = a.reshape(a.shape[1:]) if a.ndim == 3 else a.reshape(1, -1)
        shared[nm] = np.ascontiguousarray(a)
    amask_all = []
    q = np.arange(512)[None, :]
    for r in range(4):
        m = np.zeros((4, 4, 128, 512), np.float32)
        for gp in range(4):
            for kt in range(4):
                if gp < r:
                    m[gp, kt] = 1.0
                elif gp == r:
                    kk = (kt * 128 + np.arange(128))[:, None]
                    m[gp, kt] = (q >= kk).astype(np.float32)
        amask_all.append(m.reshape(16, 128, 512).astype(ml_dtypes.bfloat16))
    in_maps = []
    for c in range(8):
        b, idx = _core_tokens(c)
        m = dict(shared)
        m["xin"] = np.ascontiguousarray(np.concatenate([inp["x_prompt"][b, idx], inp["x_sample"][NS * c:NS * c + NS, 0]], 0))
        m["pin"] = np.ascontiguousarray(np.concatenate([inp["p_prompt"][0, b, idx], inp["p_sample"][0, NS * c:NS * c + NS, 0]], 0))
        m["ident"] = ident
        m.update(_core_consts(c))
        m["amask"] = amask_all[c % 4]
        m["state"] = np.ascontiguousarray(inp["state_ret"][0, NS * c:NS * c + NS])
        m["ptT"] = np.ascontiguousarray(np.asarray(inp["page_table"])[NS * c:NS * c + NS].T.astype(np.int32))
        if with_cache:
            m["cache_ckv"] = np.asarray(inp["cache_ckv"]).reshape(20480, 16384)
            m["cache_kpe"] = np.asarray(inp["cache_kpe"]).reshape(20480, 4096)
        in_maps.append(m)
    res = run_bass_kernel_spmd(nc, in_maps, core_ids=list(range(8)))
    y_p = np.zeros((2, 8192, D), np.float32)
    y_s = np.zeros((128, 1, D), np.float32)
    ckv_p = np.zeros((1, 2, 8192, 128), np.float32)
    kpe_p = np.zeros((1, 2, 8192, 32), np.float32)
    ret_p = np.zeros((1, 2, 4, 128, 128), np.float32)
    ckv_s = np.zeros((1, 128, 1, 128), np.float32)
    kpe_s = np.zeros((1, 128, 1, 32), np.float32)
    ret_s = np.zeros((1, 128, 4, 128, 128), np.float32)
    for c in range(8):
        b, idx = _core_tokens(c)
        r = res.results[c]
        y_p[b, idx] = r["y"][:TOKP]
        y_s[NS * c:NS * c + NS, 0] = r["y"][TOKP:]
        ckv_p[0, b, idx] = r["ckv_new"][:TOKP]
        ckv_s[0, NS * c:NS * c + NS, 0] = r["ckv_new"][TOKP:]
        kpe_p[0, b, idx] = r["kpe_new"][:TOKP]
        kpe_s[0, NS * c:NS * c + NS, 0] = r["kpe_new"][TOKP:]
        ret_s[0, NS * c:NS * c + NS] = r["ret_s"]
        if c % 4 == 3:
            ret_p[0, b] = r["ret_p"]
    return (y_p, y_s, ckv_p, kpe_p, ret_p, ckv_s, kpe_s, ret_s)
```
